# Optimizing a Trainium2 kernel written in Bass

```python
import jax, jax.numpy as jnp
from jax import lax
import numpy as np

D_MODEL = 2048
BATCH = 2
SEQ = 4096
DEPTH = 4

CHUNK = 64
Q_BLOCK = 128
N_MIXERS = 2
CONV_WIDTH = 31
N_HEADS = 16
QK_NOPE_DIM = 128
QK_ROPE_DIM = 64
V_HEAD_DIM = 128
Q_LORA_RANK = 512
KV_LORA_RANK = 512
D_FF = 4 * D_MODEL
ROPE_THETA = 10000.0
NORM_EPS = 1e-6
LN_EPS = 1e-5
N_CONV_LAYERS = (DEPTH + 1) // 2
N_MLA_LAYERS = DEPTH // 2

kernel_name = "hybrid_conformer_conv_mla_sqrelu_trunk"


def rms_norm(x, g):
    xf = x.astype(jnp.float32)
    y = xf * lax.rsqrt(jnp.mean(xf * xf, axis=-1, keepdims=True) + NORM_EPS)
    return (y * g.astype(jnp.float32)).astype(x.dtype)


def layer_norm(x, g, b):
    xf = x.astype(jnp.float32)
    mu = jnp.mean(xf, axis=-1, keepdims=True)
    xc = xf - mu
    var = jnp.mean(xc * xc, axis=-1, keepdims=True)
    y = xc * lax.rsqrt(var + LN_EPS) * g.astype(jnp.float32) + b.astype(jnp.float32)
    return y.astype(x.dtype)


def conv_module(h, w_pw1, b_pw1, w_dw, b_dw, ln_g, ln_b, w_pw2, b_pw2):
    u = h @ w_pw1 + b_pw1
    a, gate = jnp.split(u, 2, axis=-1)
    u = a * jax.nn.sigmoid(gate)
    u = lax.conv_general_dilated(
        u, w_dw[:, None, :].astype(u.dtype), window_strides=(1,),
        padding=[(CONV_WIDTH - 1, 0)],
        dimension_numbers=("NWC", "WIO", "NWC"),
        feature_group_count=D_MODEL) + b_dw
    u = jax.nn.silu(layer_norm(u, ln_g, ln_b))
    return u @ w_pw2 + b_pw2


def rope_tables(positions):
    inv_freq = ROPE_THETA ** (-jnp.arange(0, QK_ROPE_DIM, 2, dtype=jnp.float32) / QK_ROPE_DIM)
    ang = positions.astype(jnp.float32)[..., None] * inv_freq
    return jnp.cos(ang), jnp.sin(ang)


def apply_rope(x, cos, sin):
    xf = x.astype(jnp.float32)
    x1, x2 = jnp.split(xf, 2, axis=-1)
    out = jnp.concatenate([x1 * cos - x2 * sin, x2 * cos + x1 * sin], axis=-1)
    return out.astype(x.dtype)


def mla(h, cos, sin, w_in, q_norm_g, kv_norm_g, w_q_up, w_kv_up, w_o):
    B, S, _ = h.shape
    down = h @ w_in
    c_q, c_kv, k_pe = jnp.split(down, [Q_LORA_RANK, Q_LORA_RANK + KV_LORA_RANK], axis=-1)
    q = (rms_norm(c_q, q_norm_g) @ w_q_up).reshape(B, S, N_HEADS, QK_NOPE_DIM + QK_ROPE_DIM)
    q_nope, q_pe = jnp.split(q, [QK_NOPE_DIM], axis=-1)
    q_pe = apply_rope(q_pe, cos[:, :, None, :], sin[:, :, None, :])
    k_pe = apply_rope(k_pe, cos, sin)
    kv = (rms_norm(c_kv, kv_norm_g) @ w_kv_up).reshape(B, S, N_HEADS, QK_NOPE_DIM + V_HEAD_DIM)
    k_nope, v = jnp.split(kv, [QK_NOPE_DIM], axis=-1)

    n_blk = S // Q_BLOCK
    qn_blocks = q_nope.reshape(B, n_blk, Q_BLOCK, N_HEADS, QK_NOPE_DIM).transpose(1, 0, 2, 3, 4)
    qp_blocks = q_pe.reshape(B, n_blk, Q_BLOCK, N_HEADS, QK_ROPE_DIM).transpose(1, 0, 2, 3, 4)
    k_chunk = jnp.arange(S) // CHUNK
    scale = (QK_NOPE_DIM + QK_ROPE_DIM) ** -0.5

    def attend_block(args):
        blk, qn, qp = args
        s = (jnp.einsum('bqhd,bkhd->bhqk', qn, k_nope)
             + jnp.einsum('bqhr,bkr->bhqk', qp, k_pe)).astype(jnp.float32) * scale
        q_chunk = (blk * Q_BLOCK + jnp.arange(Q_BLOCK)) // CHUNK
        mask = k_chunk[None, :] <= q_chunk[:, None]
        s = jnp.where(mask[None, None], s, -jnp.inf)
        p = jax.nn.softmax(s, axis=-1).astype(v.dtype)
        return jnp.einsum('bhqk,bkhd->bqhd', p, v)

    o = lax.map(attend_block, (jnp.arange(n_blk), qn_blocks, qp_blocks))
    o = o.transpose(1, 0, 2, 3, 4).reshape(B, S, N_HEADS * V_HEAD_DIM)
    return o @ w_o


def sq_relu_mlp(h, w1, w2):
    return jnp.square(jax.nn.relu(h @ w1)) @ w2


def setup_inputs(seed: int = 0) -> dict:
    key = jax.random.key(seed)
    ks = iter(jax.random.split(key, 32))
    f32 = jnp.float32

    def dense(shape, fan_in):
        return jax.random.normal(next(ks), shape, f32) * (fan_in ** -0.5)

    def gain(shape):
        return 1.0 + 0.02 * jax.random.normal(next(ks), shape, f32)

    def bias(shape):
        return 0.02 * jax.random.normal(next(ks), shape, f32)

    Nc, Nm, D = N_CONV_LAYERS, N_MLA_LAYERS, D_MODEL
    x = jax.random.normal(next(ks), (BATCH, SEQ, D), f32)
    positions = jnp.broadcast_to(jnp.arange(SEQ, dtype=jnp.int32)[None, :], (BATCH, SEQ))
    return {
        "x": x,
        "positions": positions,
        "norm_mixer_g": gain((DEPTH, D)),
        "norm_mlp_g": gain((DEPTH, D)),
        "conv_w_pw1": dense((Nc, D, 2 * D), D),
        "conv_b_pw1": bias((Nc, 2 * D)),
        "conv_w_dw": dense((Nc, CONV_WIDTH, D), CONV_WIDTH),
        "conv_b_dw": bias((Nc, D)),
        "conv_ln_g": gain((Nc, D)),
        "conv_ln_b": bias((Nc, D)),
        "conv_w_pw2": dense((Nc, D, D), D),
        "conv_b_pw2": bias((Nc, D)),
        "mla_w_in": dense((Nm, D, Q_LORA_RANK + KV_LORA_RANK + QK_ROPE_DIM), D),
        "mla_q_norm_g": gain((Nm, Q_LORA_RANK)),
        "mla_kv_norm_g": gain((Nm, KV_LORA_RANK)),
        "mla_w_q_up": dense((Nm, Q_LORA_RANK, N_HEADS * (QK_NOPE_DIM + QK_ROPE_DIM)), Q_LORA_RANK),
        "mla_w_kv_up": dense((Nm, KV_LORA_RANK, N_HEADS * (QK_NOPE_DIM + V_HEAD_DIM)), KV_LORA_RANK),
        "mla_w_o": dense((Nm, N_HEADS * V_HEAD_DIM, D), N_HEADS * V_HEAD_DIM),
        "mlp_w1": dense((DEPTH, D, D_FF), D),
        "mlp_w2": dense((DEPTH, D_FF, D), D_FF),
        "final_norm_g": gain((D,)),
    }


def reference(x, positions, norm_mixer_g, norm_mlp_g,
              conv_w_pw1, conv_b_pw1, conv_w_dw, conv_b_dw, conv_ln_g, conv_ln_b,
              conv_w_pw2, conv_b_pw2,
              mla_w_in, mla_q_norm_g, mla_kv_norm_g, mla_w_q_up, mla_w_kv_up, mla_w_o,
              mlp_w1, mlp_w2, final_norm_g):
    cos, sin = rope_tables(positions)
    for layer in range(DEPTH):
        j = layer // N_MIXERS
        h = rms_norm(x, norm_mixer_g[layer])
        if layer % N_MIXERS == 0:
            x = x + conv_module(h, conv_w_pw1[j], conv_b_pw1[j], conv_w_dw[j], conv_b_dw[j],
                                conv_ln_g[j], conv_ln_b[j], conv_w_pw2[j], conv_b_pw2[j])
        else:
            x = x + mla(h, cos, sin, mla_w_in[j], mla_q_norm_g[j], mla_kv_norm_g[j],
                        mla_w_q_up[j], mla_w_kv_up[j], mla_w_o[j])
        h = rms_norm(x, norm_mlp_g[layer])
        x = x + sq_relu_mlp(h, mlp_w1[layer], mlp_w2[layer])
    return rms_norm(x, final_norm_g)
```

```python
import numpy as np
import ml_dtypes
from contextlib import ExitStack
import concourse.bass as bass
import concourse.mybir as mybir
from concourse.bass_utils import run_bass_kernel_spmd

F32 = mybir.dt.float32
BF16 = mybir.dt.bfloat16
I32 = mybir.dt.int32
AF = mybir.ActivationFunctionType
ALU = mybir.AluOpType
AX = mybir.AxisListType

NEG = -30000.0
SCALE = 192.0 ** -0.5
TWO_PI = 2.0 * np.pi
C1 = 6.28125
C2 = TWO_PI - C1

G_MIX = 0
G_MLP = 64
G_FIN = 128
CB = 144
WDW = 336
MG = 1328
RC = 1344
SEL = 1346
NPAR = 1362

DEBUG_STOP = None
DEBUG_DUMPS = ()
DEBUG_MODE = None
_ESZ = {}


def _esz(dt):
    k = str(dt)
    if k not in _ESZ:
        _ESZ[k] = 2 if ("bfloat16" in k or "float16" in k) else 4
    return _ESZ[k]


class Op:
    __slots__ = ("id", "eng", "fn", "sdeps", "wdeps", "chan", "inc", "sig", "cnt", "waits")


class Prog:
    BS = 128

    def __init__(self, nc):
        self.nc = nc
        self.ops = []
        self.tinfo = {}
        self.blk = {}
        self.chan_cnt = {}
        self.final_chans = []

    def reg(self, handle, space, base):
        self.tinfo[handle.name] = (space, base)

    def _keys(self, a):
        if isinstance(a, str):
            return [("dram", a)]
        space, base = self.tinfo[a.tensor.name]
        pat = a.ap
        row = pat[0][0]
        off = a.offset % row if row > 0 else a.offset
        span = 1
        for s, c in pat[1:]:
            span += (c - 1) * abs(s)
        es = _esz(a.dtype)
        bs = 4 if space == "st" else self.BS
        b0 = (base + off * es) // bs
        b1 = (base + (off + span) * es - 1) // bs
        return [(space, b) for b in range(b0, b1 + 1)]

    def add(self, eng, fn, reads=(), writes=(), chan=None, inc=16):
        op = Op()
        op.id = len(self.ops)
        op.eng = eng
        op.fn = fn
        op.chan = chan
        op.inc = inc
        op.sig = chan is not None
        op.cnt = 0
        sd, wd = set(), set()
        rk, wk = [], []
        for a in reads:
            rk.extend(self._keys(a))
        for a in writes:
            wk.extend(self._keys(a))
        blk = self.blk
        for k in rk:
            e = blk.get(k)
            if e is not None and e[0] >= 0:
                sd.add(e[0])
        for k in wk:
            e = blk.get(k)
            if e is not None:
                if e[0] >= 0:
                    sd.add(e[0])
                wd.update(e[1].values())
                wd.update(e[2])
        isdma = chan is not None
        for k in rk:
            e = blk.get(k)
            if e is None:
                e = [-1, {}, []]
                blk[k] = e
            if isdma:
                e[2].append(op.id)
            else:
                e[1][eng] = op.id
        for k in wk:
            blk[k] = [op.id, {}, []]
        sd.discard(op.id)
        wd.discard(op.id)
        op.sdeps = sd
        op.wdeps = wd - sd
        if chan is not None:
            self.chan_cnt[chan] = self.chan_cnt.get(chan, 0) + inc
            op.cnt = self.chan_cnt[chan]
        self.ops.append(op)
        return op

    def finalize(self, stack):
        nc = self.nc
        ops = self.ops
        for op in ops:
            best = {}
            for d, strong in [(x, True) for x in op.sdeps] + [(x, False) for x in op.wdeps]:
                p = ops[d]
                if p.chan is not None:
                    key = ("c", p.chan)
                else:
                    if p.eng == op.eng:
                        if p.eng == "pe":
                            continue
                        if not strong and op.chan is None:
                            continue
                    key = ("e", p.eng)
                if key not in best or best[key] < d:
                    best[key] = d
            op.waits = list(best.values())
            for d in op.waits:
                ops[d].sig = True
        ecnt = {}
        for op in ops:
            if op.chan is None and op.sig:
                ecnt[op.eng] = ecnt.get(op.eng, 0) + 1
                op.cnt = ecnt[op.eng]
        for e, c in ecnt.items():
            assert c < 60000, (e, c)
        sems = {}
        for e in ("pe", "act", "dve", "pool", "sp"):
            sems[("e", e)] = stack.enter_context(nc.semaphore("s_" + e))
        for ch in self.chan_cnt:
            sems[("c", ch)] = stack.enter_context(nc.semaphore("c_" + ch))
        streams = {e: [] for e in ("pe", "act", "dve", "pool", "sp")}
        for op in ops:
            streams[op.eng].append(op)

        def emit(ename, eng):
            waited = {}
            for op in streams[ename]:
                for d in op.waits:
                    p = ops[d]
                    key = ("c", p.chan) if p.chan is not None else ("e", p.eng)
                    if waited.get(key, 0) < p.cnt:
                        eng.wait_ge(sems[key], p.cnt)
                        waited[key] = p.cnt
                ins = op.fn(eng)
                if op.chan is not None:
                    ins.then_inc(sems[("c", op.chan)], op.inc)
                elif op.sig:
                    ins.then_inc(sems[("e", op.eng)], 1)
            if ename == "sp":
                for ch in self.final_chans:
                    eng.wait_ge(sems[("c", ch)], self.chan_cnt[ch])

        block = stack.enter_context(nc.Block())

        @block.tensor
        def _(e):
            emit("pe", e)

        @block.scalar
        def _(e):
            emit("act", e)

        @block.vector
        def _(e):
            emit("dve", e)

        @block.gpsimd
        def _(e):
            emit("pool", e)

        @block.sync
        def _(e):
            emit("sp", e)


class Ctx:
    pass


def build_program(stop=None):
    nc = bass.Bass("TRN2", target_bir_lowering=False)
    P = Prog(nc)
    C = Ctx()
    C.nc, C.P = nc, P

    def dram(name, shape, dt, kind):
        return nc.dram_tensor(name, list(shape), dt, kind=kind).ap()

    D = Ctx()
    D.xT = dram("xT", [128, 16, 1024], F32, "ExternalInput")
    D.params = dram("params", [128, NPAR], F32, "ExternalInput")
    D.ident = dram("ident", [128, 128], BF16, "ExternalInput")
    D.pos = dram("pos", [64, 1024], I32, "ExternalInput")
    D.maskrows = dram("maskrows", [16, 4, 1024], BF16, "ExternalInput")
    D.qsel = dram("qsel", [16, 1024], BF16, "ExternalInput")
    dbg = DEBUG_MODE is not None
    D.w1r = dram("w1r", [1, 1, 128, 16, 512] if dbg else [4, 16, 128, 16, 512], F32, "ExternalInput")
    D.w2r = dram("w2r", [1, 1, 128, 4, 2048] if dbg else [4, 16, 128, 4, 2048], F32, "ExternalInput")
    D.pw1r = dram("pw1r", [1, 1, 128, 16, 512] if dbg else [2, 8, 128, 16, 512], F32, "ExternalInput")
    D.pw2r = dram("pw2r", [1, 1, 128, 16, 512] if dbg else [2, 4, 128, 16, 512], F32, "ExternalInput")
    nj = 1 if dbg else 2
    D.winr = dram("winr", [nj, 2, 128, 16, 512], F32, "ExternalInput")
    D.wkper = dram("wkper", [nj, 128, 16, 128], F32, "ExternalInput")
    D.wqr = dram("wqr", [nj, 16, 128, 4, 256], F32, "ExternalInput")
    D.wkvr = dram("wkvr", [nj, 2, 128, 4, 2048], F32, "ExternalInput")
    D.wor = dram("wor", [nj, 4, 128, 16, 512], F32, "ExternalInput")
    D.yT = dram("yT", [128, 16, 1024], F32, "ExternalOutput")
    D.kloc = [dram("kloc%d" % g, [512, 1024], BF16, "Internal") for g in range(4)]
    D.vloc = [dram("vloc%d" % g, [512, 1024], BF16, "Internal") for g in range(4)]
    D.kall = [dram("kall%d" % g, [2048, 1024], BF16, "Internal") for g in range(4)]
    D.vall = [dram("vall%d" % g, [2048, 1024], BF16, "Internal") for g in range(4)]
    D.ploc = dram("ploc", [64, 1024], BF16, "Internal")
    D.pall = dram("pall", [256, 1024], BF16, "Internal")
    D.tl_loc = dram("tl_loc", [256, 512], BF16, "Internal")
    D.tl_all = dram("tl_all", [1024, 512], BF16, "Internal")
    D.kv2loc = [dram("kv2loc%d" % i, [512, 1024], BF16, "Internal") for i in range(2)]
    D.kv2all = [dram("kv2all%d" % i, [2048, 1024], BF16, "Internal") for i in range(2)]
    C.D = D

    TOTAL = 212000
    base, _end = nc.bump_sbuf(TOTAL)
    cnt = [0]

    def sb(shape, dt, off, space="sb"):
        cnt[0] += 1
        h = nc.alloc_sbuf_tensor_at("t%d" % cnt[0], list(shape), dt, offset=base + off)
        P.reg(h, space, off)
        return h

    XO, HO, SMO, ARO = 0, 65536, 98304, 113664
    C.X = sb([128, 16, 1024], F32, XO)
    C.H = sb([128, 16, 1024], BF16, HO)
    C.PAR = sb([128, NPAR], F32, SMO)
    C.IDENT = sb([128, 128], BF16, SMO + 5632)
    C.ONES = sb([128, 128], BF16, SMO + 5888)
    C.COS = sb([128, 1024], F32, SMO + 6144)
    C.SIN = sb([128, 1024], F32, SMO + 10240)
    C.EPS = sb([128, 2], F32, SMO + 14336)
    C.ST = sb([128, 64], F32, SMO + 14400, space="st")
    A = ARO
    NT = A + 92160
    C.SQ = [sb([128, 512], BF16, NT + i * 1024) for i in range(2)]
    C.RS = sb([128, 512], F32, NT + 2048)
    C.RSTD = sb([128, 512], F32, NT + 4096)
    C.W1V = [sb([128, 16, 512], BF16, A + s * 16384) for s in range(4)]
    C.W2V = [sb([128, 4, 2048], BF16, A + s * 16384) for s in range(4)]
    C.WK = sb([128, 16, 128], BF16, A + 32768)
    C.A1 = [sb([128, 4, 1024], BF16, A + 65536 + i * 8192) for i in range(2)]
    C.RT = [sb([128, 512], F32, A + 81920 + i * 2048) for i in range(2)]
    C.U = sb([128, 16, 2, 544], BF16, A + 32768)
    C.DG = [sb([128, 31, 128], BF16, A + 67584 + i * 7936) for i in range(2)]
    C.TL = sb([128, 8, 512], BF16, A + 67584)
    C.SG = [sb([128, 512], F32, A + 83456 + i * 2048) for i in range(2)]
    C.TC = sb([128, 2, 512], BF16, A + 83456)
    C.LNT = [sb([128, 512], F32, A + 83456 + i * 2048) for i in range(4)]
    C.CRAW = sb([128, 8, 512], F32, A + 49152)
    C.CN = sb([128, 8, 1024], BF16, A + 65536)
    C.KS = [sb([128, 1024], BF16, A + 81920 + i * 2048) for i in range(3)]
    C.RT1 = sb([128, 512], F32, A + 88064)
    C.RT2 = sb([128, 512], F32, A + 90112)
    C.KT = [sb([128, 4, 1024], BF16, A + i * 8192) for i in range(2)]
    C.VT = sb([128, 4, 1024], BF16, A + 16384)
    C.VV = sb([128, 32, 128], BF16, A + 24576)
    C.PB = [sb([128, 4096], BF16, A + 32768 + i * 8192) for i in range(2)]
    C.PT = [sb([128, 32, 128], BF16, A + 49152 + i * 8192) for i in range(2)]
    C.KPE = sb([128, 4, 1024], BF16, A + 73728)
    C.WQ = sb([128, 4, 256], BF16, A + 81920)
    C.QN = [sb([128, 1024], BF16, A + 83968 + i * 2048) for i in range(2)]
    C.QP = [sb([128, 1024], BF16, A + 88064 + i * 2048) for i in range(2)]
    C.AT1 = sb([128, 512], F32, A + 92160)
    C.AT2 = sb([128, 512], F32, A + 94208)
    C.DINV = sb([128, 8, 128], BF16, A + 96256)
    C.TMPI = sb([128, 1024], I32, A + 0)
    C.TMPA = sb([128, 1024], F32, A + 4096)
    C.TMPB = sb([128, 1024], F32, A + 8192)
    C.TMPC = sb([128, 1024], F32, A + 12288)
    C.TMPD = sb([128, 1024], F32, A + 16384)
    stack = ExitStack()
    C.PS = []
    C.PSB = []
    for b in range(8):
        h = stack.enter_context(nc.psum_tensor("ps%d" % b, [128, 512], F32))
        P.reg(h, "ps", b * 2048)
        C.PS.append(h)
        hb = h.bitcast(BF16)
        P.reg(hb, "ps", b * 2048)
        C.PSB.append(hb)

    emit_all(C, stop)
    P.finalize(stack)
    stack.close()
    return nc


def mm(C, out, lhsT, rhs, start, stop):
    C.P.add("pe", lambda e: e.matmul(out, lhsT=lhsT, rhs=rhs, start=start, stop=stop),
            reads=[lhsT, rhs], writes=[out])


def tr(C, out, in_, ident):
    C.P.add("pe", lambda e: e.transpose(out, in_, ident), reads=[in_, ident], writes=[out])


def act(C, out, in_, func, bias=None, scale=None, accum=None):
    reads = [in_]
    kw = {}
    if bias is not None:
        kw["bias"] = bias
        if not isinstance(bias, float):
            reads.append(bias)
    if scale is not None:
        kw["scale"] = scale
        if not isinstance(scale, float):
            reads.append(scale)
    writes = [out]
    if accum is not None:
        kw["accum_out"] = accum
        writes.append(accum)
    C.P.add("act", lambda e: e.activation(out=out, in_=in_, func=func, **kw), reads=reads, writes=writes)


def tt(C, out, in0, in1, op, eng="dve"):
    C.P.add(eng, lambda e: e.tensor_tensor(out=out, in0=in0, in1=in1, op=op), reads=[in0, in1], writes=[out])


def stt(C, out, in0, scalar, in1, op0, op1):
    reads = [in0, in1]
    if not isinstance(scalar, float):
        reads.append(scalar)
    C.P.add("dve", lambda e: e.scalar_tensor_tensor(out=out, in0=in0, scalar=scalar, in1=in1, op0=op0, op1=op1),
            reads=reads, writes=[out])


def ts(C, out, in0, s1, op0, s2=None, op1=None):
    reads = [in0]
    if not isinstance(s1, float):
        reads.append(s1)
    if s2 is not None and not isinstance(s2, float):
        reads.append(s2)
    if op1 is None:
        C.P.add("dve", lambda e: e.tensor_scalar(out=out, in0=in0, scalar1=s1, scalar2=None, op0=op0),
                reads=reads, writes=[out])
    else:
        C.P.add("dve", lambda e: e.tensor_scalar(out=out, in0=in0, scalar1=s1, scalar2=s2, op0=op0, op1=op1),
                reads=reads, writes=[out])


def cp(C, out, in_, eng="dve"):
    if eng == "act":
        C.P.add("act", lambda e: e.copy(out=out, in_=in_), reads=[in_], writes=[out])
    else:
        C.P.add(eng, lambda e: e.tensor_copy(out=out, in_=in_), reads=[in_], writes=[out])


def red(C, out, in_, op, negate=False):
    C.P.add("dve", lambda e: e.tensor_reduce(out=out, in_=in_, axis=AX.X, op=op, negate=negate),
            reads=[in_], writes=[out])


def recip(C, out, in_):
    C.P.add("dve", lambda e: e.reciprocal(out=out, in_=in_), reads=[in_], writes=[out])


_dma_n = [0]


def dma(C, out, in_, chan=None, reads=None, writes=None, eng="sp", cast=False):
    if chan is None:
        _dma_n[0] += 1
        chan = "d%d" % _dma_n[0]
    r = reads if reads is not None else [in_]
    w = writes if writes is not None else [out]
    if cast:
        C.P.add("pool", lambda e: e.dma_start(out=out, in_=in_, max_dma_last_dim=8192), reads=r, writes=w, chan=chan)
    else:
        C.P.add(eng, lambda e: e.dma_start(out=out, in_=in_), reads=r, writes=w, chan=chan)
    return chan


def wload(C, view, src, slot):
    dma(C, view[:], src, chan="W%d" % slot, reads=["w"], writes=[view[:]], cast=True)


def par(C, col, n=1):
    return C.PAR[:, col:col + n]


def TS(t):
    return slice(t * 512, (t + 1) * 512)


def startup(C):
    D = C.D
    for q in range(4):
        dma(C, C.X[:, 4 * q:4 * q + 4, :], D.xT[:, 4 * q:4 * q + 4, :], reads=["xin"])
    dma(C, C.PAR[:], D.params, reads=["pin"])
    dma(C, C.IDENT[:], D.ident, reads=["pin"])
    dma(C, C.TMPI[0:64, :], D.pos, reads=["pin"])
    C.P.add("dve", lambda e: e.memset(C.ONES[:], 1.0), writes=[C.ONES[:]])
    C.P.add("dve", lambda e: e.memset(C.EPS[:, 0:1], 1e-6), writes=[C.EPS[:, 0:1]])
    C.P.add("dve", lambda e: e.memset(C.EPS[:, 1:2], 1e-5), writes=[C.EPS[:, 1:2]])
    R = slice(0, 64)
    A_, B_, Cc, Dd = C.TMPA[R, :], C.TMPB[R, :], C.TMPC[R, :], C.TMPD[R, :]
    cp(C, A_, C.TMPI[R, :])
    ts(C, A_, A_, C.PAR[R, RC:RC + 1], ALU.mult)
    ts(C, B_, A_, float(1.0 / TWO_PI), ALU.mult)
    cp(C, C.TMPI[R, :], B_)
    cp(C, B_, C.TMPI[R, :])
    stt(C, Cc, B_, float(-C1), A_, ALU.mult, ALU.add)
    stt(C, Cc, B_, float(-C2), Cc, ALU.mult, ALU.add)

    def wrap(t_, tmp):
        ts(C, tmp, t_, float(np.pi), ALU.is_gt, float(-TWO_PI), ALU.mult)
        tt(C, t_, t_, tmp, ALU.add)
        ts(C, tmp, t_, float(-np.pi), ALU.is_lt, float(TWO_PI), ALU.mult)
        tt(C, t_, t_, tmp, ALU.add)

    wrap(Cc, Dd)
    act(C, B_, Cc, AF.Sin)
    ts(C, C.SIN[R, :], B_, C.PAR[R, RC + 1:RC + 2], ALU.mult)
    ts(C, Cc, Cc, float(np.pi / 2), ALU.add)
    wrap(Cc, Dd)
    act(C, C.COS[R, :], Cc, AF.Sin)


def rmsnorm(C, gcol, final=False, deferred=False):
    for t in range(2):
        ps = C.PS[7 - t] if deferred else C.PS[7]
        if not deferred:
            for c in range(16):
                sq = C.SQ[c % 2]
                if c % 2 == 0:
                    act(C, sq[:], C.X[:, c, TS(t)], AF.Square)
                else:
                    tt(C, sq[:], C.X[:, c, TS(t)], C.X[:, c, TS(t)], ALU.mult)
                mm(C, ps[:], C.ONES[:], sq[:], c == 0, c == 15)
        act(C, C.RS[:], ps[:], AF.Sqrt, bias=C.EPS[:, 0:1], scale=float(1.0 / 2048))
        recip(C, C.RSTD[:], C.RS[:])
        for c in range(16):
            dst = C.X[:, c, TS(t)] if final else C.H[:, c, TS(t)]
            stt(C, dst, C.X[:, c, TS(t)], par(C, gcol + c), C.RSTD[:], ALU.mult, ALU.mult)


class StatAcc:
    def __init__(self, C):
        self.C = C
        self.pend = None
        self.n = 0

    def add(self, c, t):
        C = self.C
        sq = C.SQ[self.n % 2]
        self.n += 1
        act(C, sq[:], C.X[:, c, TS(t)], AF.Square)
        prev = self.pend
        self.pend = (c, t, sq)
        if prev is not None:
            self._mm(prev)

    def _mm(self, p):
        c, t, sq = p
        mm(self.C, self.C.PS[7 - t][:], self.C.ONES[:], sq[:], c == 0, c == 15)

    def flush(self):
        if self.pend is not None:
            self._mm(self.pend)
            self.pend = None


def mlp(C, L):
    D = C.D

    def load(g):
        wload(C, C.W1V[(2 * g) % 4], D.w1r[L, g], (2 * g) % 4)
        wload(C, C.W2V[(2 * g + 1) % 4], D.w2r[L, g], (2 * g + 1) % 4)

    load(0)
    rmsnorm(C, G_MLP + 16 * L, deferred=True)
    sacc = StatAcc(C)
    k1 = [0]
    k2 = [0]

    def w1part(g):
        w1 = C.W1V[(2 * g) % 4]
        A1 = C.A1[g % 2]
        for fc in range(4):
            for t in range(2):
                ps = C.PS[k1[0] % 3]
                rt = C.RT[k1[0] % 2]
                k1[0] += 1
                for kc in range(16):
                    mm(C, ps[:], w1[:, kc, fc * 128:(fc + 1) * 128], C.H[:, kc, TS(t)], kc == 0, kc == 15)
                act(C, rt[:], ps[:], AF.Relu)
                act(C, A1[:, fc, TS(t)], rt[:], AF.Square)

    def w2part(g):
        w2 = C.W2V[(2 * g + 1) % 4]
        A1 = C.A1[g % 2]
        for d2 in range(16):
            for t in range(2):
                ps = C.PS[3 + k2[0] % 3]
                k2[0] += 1
                for kc in range(4):
                    mm(C, ps[:], w2[:, kc, d2 * 128:(d2 + 1) * 128], A1[:, kc, TS(t)], kc == 0, kc == 3)
                tt(C, C.X[:, d2, TS(t)], C.X[:, d2, TS(t)], ps[:], ALU.add)
                if g == 15:
                    sacc.add(d2, t)
        if g == 15:
            sacc.flush()

    wload(C, C.W1V[2], D.w1r[L, 1], 2)
    w1part(0)
    for g in range(16):
        if g + 1 < 16:
            wload(C, C.W2V[(2 * (g + 1) + 1) % 4], D.w2r[L, g + 1], (2 * (g + 1) + 1) % 4)
            w1part(g + 1)
        if g + 2 < 16:
            wload(C, C.W1V[(2 * (g + 2)) % 4], D.w1r[L, g + 2], (2 * (g + 2)) % 4)
        w2part(g)


def proj_residual(C, wsrc, bias_col):
    wload(C, C.W1V[0], wsrc[0], 0)
    sacc = StatAcc(C)
    k = 0
    for n in range(4):
        if n + 1 < 4:
            wload(C, C.W1V[(n + 1) % 2], wsrc[n + 1], (n + 1) % 2)
        w = C.W1V[n % 2]
        for t in range(2):
            for dd in range(4):
                d2 = 4 * n + dd
                ps = C.PS[k % 4]
                k += 1
                for kc in range(16):
                    mm(C, ps[:], w[:, kc, dd * 128:(dd + 1) * 128], C.H[:, kc, TS(t)], kc == 0, kc == 15)
                if bias_col is None:
                    tt(C, C.X[:, d2, TS(t)], C.X[:, d2, TS(t)], ps[:], ALU.add)
                else:
                    stt(C, C.X[:, d2, TS(t)], ps[:], par(C, bias_col + d2), C.X[:, d2, TS(t)], ALU.add, ALU.add)
                sacc.add(d2, t)
    sacc.flush()


def allgather(C, src, dst, skeys, dkey, name):
    if isinstance(skeys, str):
        skeys = [skeys]
    C.P.add("pool", lambda e: e.collective_compute("AllGather", ALU.bypass,
                                                   replica_groups=[[0, 1, 2, 3], [4, 5, 6, 7]],
                                                   ins=[src], outs=[dst]),
            reads=list(skeys), writes=[dkey], chan=name, inc=1)


def conv_a(C, L):
    D = C.D
    j = L // 2
    cb = CB + 96 * j
    wload(C, C.W1V[0], D.pw1r[j, 0], 0)
    wload(C, C.W1V[1], D.pw1r[j, 1], 1)
    rmsnorm(C, G_MIX + 16 * L, deferred=(L > 0))
    k = 0
    for m in range(8):
        if 1 <= m and m + 1 < 8:
            wload(C, C.W1V[(m + 1) % 2], D.pw1r[j, m + 1], (m + 1) % 2)
        w = C.W1V[m % 2]
        for cc in range(2):
            c = 2 * m + cc
            for t in range(2):
                pa = C.PS[k % 2]
                pg = C.PS[2 + k % 2]
                sg = C.SG[k % 2]
                k += 1
                for kc in range(16):
                    mm(C, pa[:], w[:, kc, cc * 128:(cc + 1) * 128], C.H[:, kc, TS(t)], kc == 0, kc == 15)
                for kc in range(16):
                    mm(C, pg[:], w[:, kc, 256 + cc * 128:256 + (cc + 1) * 128], C.H[:, kc, TS(t)], kc == 0, kc == 15)
                act(C, sg[:], pg[:], AF.Sigmoid, bias=par(C, cb + 16 + c))
                stt(C, C.U[:, c, t, 32:544], pa[:], par(C, cb + c), sg[:], ALU.add, ALU.mult)
    for t in range(2):
        cp(C, C.TC[:, t, :].rearrange("p (c j) -> p c j", j=32), C.U[:, :, t, 512:544])
    dma(C, D.tl_loc.rearrange("(s p) n -> p s n", p=128), C.TC[:], writes=["tl_loc"])
    allgather(C, D.tl_loc, D.tl_all, "tl_loc", "tl_all", "agt")


def conv_b(C, L):
    D = C.D
    j = L // 2
    cb = CB + 96 * j
    dma(C, C.TL[:], D.tl_all.rearrange("(r p) n -> p r n", p=128), reads=["tl_all"], chan="tlld")
    for t in range(2):
        halo = C.U[:, :, t, 0:32]
        for cand in range(8):
            src = C.TL[:, cand, :].rearrange("p (c j) -> p c j", j=32)
            sc = par(C, SEL + 8 * t + cand)
            if cand == 0:
                ts(C, halo, src, sc, ALU.mult)
            else:
                stt(C, halo, src, sc, halo, ALU.mult, ALU.add)
    statb = [(C.PS[6], C.PS[7]), (C.PS[2], C.PS[3])]
    nparr = NPAR
    mu, var, nmr, t1 = C.LNT[0], C.LNT[1], C.LNT[2], C.LNT[3]

    def stat_mm(p):
        c0, t0, sq0 = p
        mm(C, statb[t0][0][:], C.ONES[:], C.H[:, c0, TS(t0)], c0 == 0, c0 == 15)
        mm(C, statb[t0][1][:], C.ONES[:], sq0[:], c0 == 0, c0 == 15)

    def ln_math(t):
        s1, s2 = statb[t]
        act(C, mu[:], s1[:], AF.Copy, scale=float(1.0 / 2048))
        tt(C, var[:], mu[:], mu[:], ALU.mult)
        stt(C, var[:], s2[:], float(1.0 / 2048), var[:], ALU.mult, ALU.subtract)
        act(C, C.RS[:], var[:], AF.Sqrt, bias=C.EPS[:, 1:2])
        recip(C, C.RSTD[:], C.RS[:])
        stt(C, nmr[:], mu[:], float(-1.0), C.RSTD[:], ALU.mult, ALU.mult)

    def normalize(t, c):
        tmp = t1 if c % 2 == 0 else var
        tt(C, tmp[:], C.H[:, c, TS(t)], C.RSTD[:], ALU.mult)
        tt(C, tmp[:], tmp[:], nmr[:], ALU.add)
        act(C, C.H[:, c, TS(t)], tmp[:], AF.Silu, bias=par(C, cb + 64 + c), scale=par(C, cb + 48 + c))

    kk = 0
    for t in range(2):
        pend = None
        for c in range(16):
            dg = C.DG[kk % 2]
            woff = WDW + (j * 16 + c) * 31
            idb = bass.AP(C.IDENT[:].tensor, C.IDENT[:].offset, [[128, 128], [0, 31], [1, 128]])
            wv = C.PAR[:, woff:woff + 31]
            wb = bass.AP(wv.tensor, wv.offset, [[nparr, 128], [1, 31], [0, 128]])
            C.P.add("dve", (lambda dg_, idb_, wb_: (lambda e: e.tensor_tensor(out=dg_[:], in0=idb_, in1=wb_, op=ALU.mult)))(dg, idb, wb),
                    reads=[C.IDENT[:], wv], writes=[dg[:]])
            ps = C.PS[4 + kk % 2]
            sq = C.SQ[kk % 2]
            kk += 1
            for k in range(31):
                mm(C, ps[:], dg[:, k, :], C.U[:, c, t, 2 + k:2 + k + 512], k == 0, k == 30)
            ts(C, C.H[:, c, TS(t)], ps[:], par(C, cb + 32 + c), ALU.add)
            act(C, sq[:], C.H[:, c, TS(t)], AF.Square)
            if pend is not None:
                stat_mm(pend)
            pend = (c, t, sq)
            if t == 1:
                normalize(0, c)
        stat_mm(pend)
        if t == 0:
            ln_math(0)
    ln_math(1)
    for c in range(16):
        normalize(1, c)
    proj_residual(C, D.pw2r[j], cb + 80)


def sub_rmsnorm(C, t, o0, gcol, psb):
    for i in range(4):
        sq = C.SQ[i % 2]
        act(C, sq[:], C.CRAW[:, o0 + i, :], AF.Square)
        mm(C, psb[:], C.ONES[:], sq[:], i == 0, i == 3)
    act(C, C.RS[:], psb[:], AF.Sqrt, bias=C.EPS[:, 0:1], scale=float(1.0 / 512))
    recip(C, C.RSTD[:], C.RS[:])
    for i in range(4):
        stt(C, C.CN[:, o0 + i, TS(t)], C.CRAW[:, o0 + i, :], par(C, gcol + i), C.RSTD[:], ALU.mult, ALU.mult)


def mla_a(C, L):
    D = C.D
    j = L // 2
    wload(C, C.W1V[0], D.winr[j, 0], 0)
    wload(C, C.W1V[1], D.winr[j, 1], 1)
    dma(C, C.WK[:], D.wkper[j], chan="W2", reads=["w"], writes=[C.WK[:]], cast=True)
    rmsnorm(C, G_MIX + 16 * L, deferred=(L > 0 and DEBUG_MODE is None))
    kc_ = [0]

    def win_chunks(t, o_list):
        for o in o_list:
            w = C.W1V[o // 4]
            oc = o % 4
            ps = C.PS[kc_[0] % 3]
            kc_[0] += 1
            for kc in range(16):
                mm(C, ps[:], w[:, kc, oc * 128:(oc + 1) * 128], C.H[:, kc, TS(t)], kc == 0, kc == 15)
            cp(C, C.CRAW[:, o, :], ps[:], eng="act")

    for t in range(2):
        px, pr = C.PS[3], C.PS[4]
        for kc in range(16):
            mm(C, px[0:64, :], C.WK[:, kc, 0:64], C.H[:, kc, TS(t)], kc == 0, kc == 15)
        for kc in range(16):
            mm(C, pr[0:64, :], C.WK[:, kc, 64:128], C.H[:, kc, TS(t)], kc == 0, kc == 15)
        tt(C, C.RT1[0:64, :], px[0:64, :], C.COS[0:64, TS(t)], ALU.mult)
        tt(C, C.RT2[0:64, :], pr[0:64, :], C.SIN[0:64, TS(t)], ALU.mult)
        tt(C, C.KS[2][0:64, TS(t)], C.RT1[0:64, :], C.RT2[0:64, :], ALU.add)
    dma(C, D.ploc, C.KS[2][0:64, :], writes=["ploc"], chan="ks2")
    allgather(C, D.ploc, D.pall, "ploc", "pall", "agp")
    wload(C, C.W2V[2], D.wkvr[j, 0], 2)
    for t in range(2):
        win_chunks(t, range(4, 8))
        sub_rmsnorm(C, t, 4, MG + 8 + 4 * j, C.PS[7])
    cnt = {"k": 0, "n": 0}

    def kv_one(h, which, w, dst_ap, key):
        hc = (h % 8) * 256
        ks = C.KS[cnt["n"] % 2]
        for t in range(2):
            ps = C.PS[cnt["k"] % 3]
            cnt["k"] += 1
            for kc in range(4):
                mm(C, ps[:], w[:, kc, hc + which * 128:hc + (which + 1) * 128], C.CN[:, 4 + kc, TS(t)], kc == 0, kc == 3)
            cp(C, ks[:, TS(t)], ps[:], eng=("act" if (cnt["k"] % 2) else "dve"))
        dma(C, dst_ap, ks[:], writes=[key], chan="ks%d" % (cnt["n"] % 2))
        cnt["n"] += 1

    def kv_group(g4):
        w = C.W2V[2] if g4 < 2 else C.W2V[0]
        if g4 == 0:
            for c2 in range(2):
                keys = []
                for which in range(2):
                    for hh in range(2):
                        h = 2 * c2 + hh
                        r0 = which * 256 + hh * 128
                        key = "kv2loc%d_%d" % (c2, which * 2 + hh)
                        keys.append(key)
                        kv_one(h, which, w, D.kv2loc[c2][r0:r0 + 128, :], key)
                allgather(C, D.kv2loc[c2], D.kv2all[c2], keys, "kv2all%d" % c2, "agk")
            return
        for which in range(2):
            for hh in range(4):
                h = 4 * g4 + hh
                dst = (D.kloc if which == 0 else D.vloc)[g4]
                kv_one(h, which, w, dst[hh * 128:(hh + 1) * 128, :], "%sloc%d_%d" % ("kv"[which], g4, hh))
            if which == 0:
                allgather(C, D.kloc[g4], D.kall[g4], ["kloc%d_%d" % (g4, q) for q in range(4)], "kall%d" % g4, "agk")
            else:
                allgather(C, D.vloc[g4], D.vall[g4], ["vloc%d_%d" % (g4, q) for q in range(4)], "vall%d" % g4, "agk")

    kv_group(0)
    for t in range(2):
        win_chunks(t, range(0, 4))
        sub_rmsnorm(C, t, 0, MG + 4 * j, C.PS[6])
    wload(C, C.W2V[0], D.wkvr[j, 1], 0)
    for g4 in range(1, 4):
        kv_group(g4)


def q_head(C, L, h, step=None):
    D = C.D
    j = L // 2
    groups = []

    def g_load():
        dma(C, C.WQ[:], D.wqr[j, h], chan="WQ", reads=["w"], writes=[C.WQ[:]], cast=True)

    def g_nope(t):
        ps = C.PS[7]
        for kc in range(4):
            mm(C, ps[:], C.WQ[:, kc, 0:128], C.CN[:, kc, TS(t)], kc == 0, kc == 3)
        act(C, C.QN[h % 2][:, TS(t)], ps[:], AF.Copy, scale=float(SCALE))

    def g_pe(t):
        px, pr = C.PS[6], C.PS[7]
        for kc in range(4):
            mm(C, px[0:64, :], C.WQ[:, kc, 128:192], C.CN[:, kc, TS(t)], kc == 0, kc == 3)
        for kc in range(4):
            mm(C, pr[0:64, :], C.WQ[:, kc, 192:256], C.CN[:, kc, TS(t)], kc == 0, kc == 3)
        stt(C, C.AT1[0:64, :], px[0:64, :], float(SCALE), C.COS[0:64, TS(t)], ALU.mult, ALU.mult)
        stt(C, C.AT2[0:64, :], pr[0:64, :], float(SCALE), C.SIN[0:64, TS(t)], ALU.mult, ALU.mult)
        tt(C, C.QP[h % 2][0:64, TS(t)], C.AT1[0:64, :], C.AT2[0:64, :], ALU.add)

    groups.append(g_load)
    for t in range(2):
        groups.append(lambda t=t: g_nope(t))
        groups.append(lambda t=t: g_pe(t))
    return groups


def attention(C, L):
    D = C.D
    def ksrc(h):
        if h < 4:
            return D.kv2all[h // 2].rearrange("(r n) t -> n r t", r=4)[(h % 2) * 128:(h % 2 + 1) * 128]
        return D.kall[h // 4].rearrange("(r n) t -> n r t", r=4)[(h % 4) * 128:(h % 4 + 1) * 128]

    def vsrc(h):
        if h < 4:
            return D.kv2all[h // 2].rearrange("(r n) t -> n r t", r=4)[256 + (h % 2) * 128:256 + (h % 2 + 1) * 128]
        return D.vall[h // 4].rearrange("(r n) t -> n r t", r=4)[(h % 4) * 128:(h % 4 + 1) * 128]

    def kkey(h):
        return ("kv2all%d" % (h // 2)) if h < 4 else ("kall%d" % (h // 4))

    def vkey(h):
        return ("kv2all%d" % (h // 2)) if h < 4 else ("vall%d" % (h // 4))

    dma(C, C.KPE[64:80, :, :], D.maskrows, reads=["pin"], chan="mrow")
    dma(C, C.QP[0][64:80, :], D.qsel, reads=["pin"], chan="qsel0")
    dma(C, C.QP[1][64:80, :], D.qsel, reads=["pin"], chan="qsel1")
    for g in q_head(C, L, 0):
        g()
    dma(C, C.KPE[0:64, :, :], D.pall.rearrange("(r n) t -> n r t", r=4), reads=["pall"], chan="kpe")
    dma(C, C.KT[0][:], ksrc(0), reads=[kkey(0)], chan="kt0")
    dma(C, C.VT[:], vsrc(0), reads=[vkey(0)], chan="vt")
    sctr = [0]
    ictr = [0]
    octr = [0]
    ST = C.ST
    iters = [(h, tile, qt) for h in range(16) for tile in range(2) for qt in range(4)]

    def nstage(itx):
        return 4 if iters[itx][1] == 0 else 8

    def rs_of(tile, s):
        return (s, 0) if tile == 0 else (s // 2, s % 2)

    def S_stage(itx, s):
        h, tile, qt = iters[itx]
        pbi = itx % 2
        qcol = slice(tile * 512 + qt * 128, tile * 512 + (qt + 1) * 128)
        kt = C.KT[h % 2]
        pb = C.PB[pbi]
        so = 16 * pbi
        r, slot = rs_of(tile, s)
        ps = C.PS[sctr[0] % 3]
        sctr[0] += 1
        mm(C, ps[:], C.QN[h % 2][:, qcol], kt[:, r, slot * 512:(slot + 1) * 512], True, False)
        mm(C, ps[:], C.QP[h % 2][0:80, qcol], C.KPE[0:80, r, slot * 512:(slot + 1) * 512], False, True)
        red(C, ST[:, so + s:so + s + 1], ps[:], ALU.max, negate=True)
        act(C, pb[:, s * 512:(s + 1) * 512], ps[:], AF.Exp, bias=ST[:, so + s:so + s + 1],
            accum=ST[:, so + 8 + s:so + 8 + s + 1])

    def combine(itx):
        ns = nstage(itx)
        pbi = itx % 2
        so = 16 * pbi
        negm = ST[:, so:so + ns]
        lsum = ST[:, so + 8:so + 8 + ns]
        o2 = 32 + 12 * pbi
        nmg = ST[:, o2:o2 + 1]
        ee = ST[:, o2 + 2:o2 + 2 + ns]
        red(C, nmg, negm, ALU.min)
        act(C, ee, negm, AF.Exp, bias=nmg, scale=-1.0)
        tt(C, lsum, lsum, ee, ALU.mult)
        red(C, ST[:, o2 + 1:o2 + 2], lsum, ALU.add)
        recip(C, ST[:, o2 + 1:o2 + 2], ST[:, o2 + 1:o2 + 2])
        ts(C, ee, ee, ST[:, o2 + 1:o2 + 2], ALU.mult)

    def build_dinv(itx):
        ns = nstage(itx)
        pbi = itx % 2
        o2 = 32 + 12 * pbi
        ee = ST[:, o2 + 2:o2 + 2 + ns]
        idv = C.IDENT[:]
        idb = bass.AP(idv.tensor, idv.offset, [[128, 128], [0, ns], [1, 128]])
        eb = bass.AP(ee.tensor, ee.offset, [[64, 128], [1, ns], [0, 128]])
        dv = C.DINV[:, 0:ns, :]
        C.P.add("dve", lambda e: e.tensor_tensor(out=dv, in0=idb, in1=eb, op=ALU.mult), reads=[idv, ee], writes=[dv])

    def T_stage(itx, s):
        pbi = itx % 2
        pb = C.PB[pbi]
        pt = C.PT[pbi]
        ps = C.PS[3 + (ictr[0] % 2)]
        ictr[0] += 1
        for sub in range(4):
            mm(C, ps[:, sub * 128:(sub + 1) * 128], pb[:, s * 512 + sub * 128:s * 512 + (sub + 1) * 128],
               C.DINV[:, s, :], True, True)
        cp(C, pt[:, 4 * s:4 * s + 4, :], ps[:].rearrange("p (a b) -> p a b", b=128),
           eng=("dve" if s % 4 == 0 else "act"))

    def PV_stage(itx, s, po):
        h, tile, qt = iters[itx]
        pt = C.PT[itx % 2]
        ns = nstage(itx)
        r, slot = rs_of(tile, s)
        for sub in range(4):
            kt_ = 4 * s + sub
            vidx = r * 8 + slot * 4 + sub
            mm(C, po, C.VV[:, vidx, :], pt[:, kt_, :], kt_ == 0, kt_ == 4 * ns - 1)

    def vtrans(h):
        for r in range(4):
            pv = C.PSB[6]
            for q8 in range(8):
                tr(C, pv[:, q8 * 128:(q8 + 1) * 128], C.VT[:, r, q8 * 128:(q8 + 1) * 128], C.IDENT[:])
            cp(C, C.VV[:, r * 8:(r + 1) * 8, :], pv[:].rearrange("p (a b) -> p a b", b=128),
               eng=("act" if r % 2 else "dve"))
        if h + 1 < 16:
            dma(C, C.VT[:], vsrc(h + 1), reads=[vkey(h + 1)], chan="vt")

    dma(C, C.KT[1][:], ksrc(1), reads=[kkey(1)], chan="kt1")
    vtrans(0)
    for s in range(nstage(0)):
        S_stage(0, s)
    combine(0)
    build_dinv(0)
    qg = q_head(C, L, 1)
    N = len(iters)
    for itx in range(N):
        h, tile, qt = iters[itx]
        first_of_head = (tile == 0 and qt == 0)
        if first_of_head and h >= 1:
            vtrans(h)
            if h + 1 < 16:
                dma(C, C.KT[(h + 1) % 2][:], ksrc(h + 1), reads=[kkey(h + 1)], chan="kt%d" % ((h + 1) % 2))
                qg = q_head(C, L, h + 1)
            else:
                qg = []
        ns = nstage(itx)
        nn = nstage(itx + 1) if itx + 1 < N else 0
        oslot = octr[0] % 4
        octr[0] += 1
        po = C.PS[5][:, oslot * 128:(oslot + 1) * 128]
        LAG = 2
        SL = 1
        for s in range(max(ns + LAG + SL, nn)):
            if s < nn:
                S_stage(itx + 1, s)
                if s == nn - 1:
                    combine(itx + 1)
            if SL <= s < ns + SL:
                T_stage(itx, s - SL)
            if LAG + SL <= s < ns + LAG + SL:
                PV_stage(itx, s - LAG - SL, po)
        cp(C, C.H[:, h, tile * 512 + qt * 128:tile * 512 + (qt + 1) * 128], po, eng="act")
        if itx + 1 < N:
            build_dinv(itx + 1)
        li = tile * 4 + qt
        if 1 <= li <= len(qg):
            qg[li - 1]()


def mla_b(C, L):
    D = C.D
    j = L // 2
    attention(C, L)
    proj_residual(C, D.wor[j], None)


def emit_all(C, stop=None):
    D = C.D
    phases = []
    startup(C)
    seq = [
        lambda: conv_a(C, 0), lambda: conv_b(C, 0), lambda: mlp(C, 0),
        lambda: mla_a(C, 1), lambda: mla_b(C, 1), lambda: mlp(C, 1),
        lambda: conv_a(C, 2), lambda: conv_b(C, 2), lambda: mlp(C, 2),
        lambda: mla_a(C, 3), lambda: mla_b(C, 3), lambda: mlp(C, 3),
    ]
    if DEBUG_MODE == "mla":
        seq = [lambda: mla_a(C, 1), lambda: mla_b(C, 1)]
        stop = 2
    n = len(seq) if stop is None else stop
    for idx, f in enumerate(seq[:n]):
        f()
        if (idx + 1) in DEBUG_DUMPS:
            dd = C.nc.dram_tensor("dbg%d" % (idx + 1), [128, 16, 1024], F32, kind="ExternalOutput").ap()
            for q in range(4):
                ch = dma(C, dd[:, 4 * q:4 * q + 4, :], C.X[:, 4 * q:4 * q + 4, :], writes=["dbgout%d" % idx])
                C.P.final_chans.append(ch)
    if stop is None:
        rmsnorm(C, G_FIN, final=True, deferred=True)
    for q in range(4):
        ch = dma(C, D.yT[:, 4 * q:4 * q + 4, :], C.X[:, 4 * q:4 * q + 4, :], writes=["yout"])
        C.P.final_chans.append(ch)


def _core_tokens(i):
    a = np.arange(i * 512, (i + 1) * 512)
    b = np.arange((7 - i) * 512, (8 - i) * 512)
    return np.concatenate([a, b])


def prepare_inputs(inp):
    f = np.float32
    x = np.asarray(inp["x"], f)
    pos = np.asarray(inp["positions"]).astype(np.int32)
    shared = {}
    w1 = np.asarray(inp["mlp_w1"], f)
    shared["w1r"] = np.ascontiguousarray(w1.reshape(4, 16, 128, 16, 512).transpose(0, 3, 2, 1, 4))
    w2 = np.asarray(inp["mlp_w2"], f)
    shared["w2r"] = np.ascontiguousarray(w2.reshape(4, 16, 4, 128, 2048).transpose(0, 1, 3, 2, 4))
    pw1 = np.asarray(inp["conv_w_pw1"], f)
    t = pw1.reshape(2, 16, 128, 2, 8, 256).transpose(0, 4, 2, 1, 3, 5)
    shared["pw1r"] = np.ascontiguousarray(t).reshape(2, 8, 128, 16, 512)
    pw2 = np.asarray(inp["conv_w_pw2"], f)
    shared["pw2r"] = np.ascontiguousarray(pw2.reshape(2, 16, 128, 4, 512).transpose(0, 3, 2, 1, 4))
    win = np.asarray(inp["mla_w_in"], f)
    t = win[:, :, :1024].reshape(2, 16, 128, 2, 512).transpose(0, 3, 2, 1, 4)
    shared["winr"] = np.ascontiguousarray(t)
    kp = win[:, :, 1024:1088]
    kpp = np.concatenate([kp, kp[:, :, 32:64], kp[:, :, 0:32]], axis=2)
    shared["wkper"] = np.ascontiguousarray(kpp.reshape(2, 16, 128, 128).transpose(0, 2, 1, 3))
    wq = np.asarray(inp["mla_w_q_up"], f).reshape(2, 4, 128, 16, 192)
    wq2 = np.concatenate([wq, wq[..., 160:192], wq[..., 128:160]], axis=-1)
    shared["wqr"] = np.ascontiguousarray(wq2.transpose(0, 3, 2, 1, 4))
    wkv = np.asarray(inp["mla_w_kv_up"], f).reshape(2, 4, 128, 2, 2048)
    shared["wkvr"] = np.ascontiguousarray(wkv.transpose(0, 3, 2, 1, 4))
    wo = np.asarray(inp["mla_w_o"], f)
    shared["wor"] = np.ascontiguousarray(wo.reshape(2, 16, 128, 4, 512).transpose(0, 3, 2, 1, 4))
    shared["ident"] = np.eye(128, dtype=f).astype(ml_dtypes.bfloat16)
    qsel = np.zeros((16, 1024), f)
    for qi in range(8):
        for hf in range(2):
            qsel[2 * qi + hf, qi * 128 + hf * 64: qi * 128 + hf * 64 + 64] = 1.0
    shared["qsel"] = qsel.astype(ml_dtypes.bfloat16)

    def fm(v):
        return np.asarray(v, f).reshape(16, 128).T

    base = np.zeros((128, NPAR), f)
    for L in range(4):
        base[:, G_MIX + 16 * L:G_MIX + 16 * L + 16] = fm(inp["norm_mixer_g"][L])
        base[:, G_MLP + 16 * L:G_MLP + 16 * L + 16] = fm(inp["norm_mlp_g"][L])
    base[:, G_FIN:G_FIN + 16] = fm(inp["final_norm_g"])
    for j in range(2):
        cb = CB + 96 * j
        b1 = np.asarray(inp["conv_b_pw1"][j], f)
        base[:, cb:cb + 16] = fm(b1[:2048])
        base[:, cb + 16:cb + 32] = fm(b1[2048:])
        base[:, cb + 32:cb + 48] = fm(inp["conv_b_dw"][j])
        base[:, cb + 48:cb + 64] = fm(inp["conv_ln_g"][j])
        base[:, cb + 64:cb + 80] = fm(inp["conv_ln_b"][j])
        base[:, cb + 80:cb + 96] = fm(inp["conv_b_pw2"][j])
        wd = np.asarray(inp["conv_w_dw"][j], f)
        base[:, WDW + j * 496:WDW + (j + 1) * 496] = wd.reshape(31, 16, 128).transpose(2, 1, 0).reshape(128, 496)
        base[:, MG + 4 * j:MG + 4 * j + 4] = np.asarray(inp["mla_q_norm_g"][j], f).reshape(4, 128).T
        base[:, MG + 8 + 4 * j:MG + 8 + 4 * j + 4] = np.asarray(inp["mla_kv_norm_g"][j], f).reshape(4, 128).T
    invf = (np.float32(10000.0) ** (-np.arange(0, 64, 2, dtype=f) / np.float32(64))).astype(f)
    base[0:64, RC] = np.concatenate([invf, invf])
    base[0:64, RC + 1] = np.concatenate([-np.ones(32, f), np.ones(32, f)])

    if DEBUG_MODE is not None:
        for k in ("w1r", "w2r", "pw1r", "pw2r"):
            shared[k] = np.ascontiguousarray(shared[k][:1, :1])
        for k in ("winr", "wkper", "wqr", "wkvr", "wor"):
            shared[k] = np.ascontiguousarray(shared[k][:1])
    maps = []
    for core in range(8):
        b, i = core // 4, core % 4
        tok = _core_tokens(i)
        m = dict(shared)
        m["xT"] = np.ascontiguousarray(x[b, tok, :].T.reshape(16, 128, 1024).transpose(1, 0, 2))
        m["pos"] = np.ascontiguousarray(np.broadcast_to(pos[b, tok][None, :], (64, 1024))).astype(np.int32)
        pr = base.copy()
        selA = np.zeros(8, f)
        selB = np.zeros(8, f)
        if i >= 1:
            selA[(i - 1) * 2 + 0] = 1.0
        if i < 3:
            selB[(i + 1) * 2 + 1] = 1.0
        else:
            selB[3 * 2 + 0] = 1.0
        pr[:, SEL:SEL + 8] = selA[None, :]
        pr[:, SEL + 8:SEL + 16] = selB[None, :]
        m["params"] = pr
        ktok = np.zeros((4, 1024), np.int64)
        for r in range(4):
            ktok[r] = _core_tokens(r)
        kch = ktok // 64
        mr = np.zeros((16, 4, 1024), f)
        for qi in range(8):
            for hf in range(2):
                qtok = tok[qi * 128 + hf * 64]
                qch = qtok // 64
                mr[2 * qi + hf] = np.where(kch > qch, NEG, 0.0)
        m["maskrows"] = mr.astype(ml_dtypes.bfloat16)
        maps.append(m)
    return maps


_NC_CACHE = {}


def _gather(res, name):
    out = np.zeros((2, 4096, 2048), np.float32)
    for core in range(8):
        b, i = core // 4, core % 4
        tok = _core_tokens(i)
        y = np.asarray(res.results[core][name])
        out[b, tok, :] = y.transpose(2, 1, 0).reshape(1024, 2048)
    return out


_LAST = {}


def kernel(**inputs):
    import time, sys
    t0 = time.time()
    maps = prepare_inputs(inputs)
    t1 = time.time()
    if "nc" not in _NC_CACHE:
        _NC_CACHE["nc"] = build_program(DEBUG_STOP)
    nc = _NC_CACHE["nc"]
    t2 = time.time()
    if DEBUG_MODE is not None:
        res = run_bass_kernel_spmd(nc, maps, core_ids=list(range(8)), trace=True)
        print("DEBUG exec_time_ns", res.exec_time_ns, file=sys.stderr)
    else:
        res = run_bass_kernel_spmd(nc, maps, core_ids=list(range(8)))
    t3 = time.time()
    print("kernel timing: prep %.1f build %.1f run %.1f" % (t1 - t0, t2 - t1, t3 - t2), file=sys.stderr)
    _LAST["res"] = res
    return _gather(res, "yT")
```

```python
import numpy as np
import ml_dtypes
from contextlib import ExitStack
import concourse.bass as bass
import concourse.mybir as mybir
from concourse.bass_utils import run_bass_kernel_spmd

F32 = mybir.dt.float32
BF16 = mybir.dt.bfloat16
I32 = mybir.dt.int32
AF = mybir.ActivationFunctionType
ALU = mybir.AluOpType
AX = mybir.AxisListType

NEG = -30000.0
SCALE = 192.0 ** -0.5
TWO_PI = 2.0 * np.pi
C1 = 6.28125
C2 = TWO_PI - C1

G_MIX = 0
G_MLP = 64
G_FIN = 128
CB = 144
WDW = 336
MG = 1328
RC = 1344
SEL = 1346
NPAR = 1362

DEBUG_STOP = None
DEBUG_DUMPS = ()
DEBUG_MODE = None
_ESZ = {}


def _esz(dt):
    k = str(dt)
    if k not in _ESZ:
        _ESZ[k] = 2 if ("bfloat16" in k or "float16" in k) else 4
    return _ESZ[k]


class Op:
    __slots__ = ("id", "eng", "fn", "sdeps", "wdeps", "chan", "inc", "sig", "cnt", "waits")


class Prog:
    BS = 128

    def __init__(self, nc):
        self.nc = nc
        self.ops = []
        self.tinfo = {}
        self.blk = {}
        self.chan_cnt = {}
        self.final_chans = []

    def reg(self, handle, space, base):
        self.tinfo[handle.name] = (space, base)

    def _keys(self, a):
        if isinstance(a, str):
            return [("dram", a)]
        space, base = self.tinfo[a.tensor.name]
        pat = a.ap
        row = pat[0][0]
        off = a.offset % row if row > 0 else a.offset
        span = 1
        for s, c in pat[1:]:
            span += (c - 1) * abs(s)
        es = _esz(a.dtype)
        bs = 4 if space == "st" else self.BS
        b0 = (base + off * es) // bs
        b1 = (base + (off + span) * es - 1) // bs
        return [(space, b) for b in range(b0, b1 + 1)]

    def add(self, eng, fn, reads=(), writes=(), chan=None, inc=16):
        op = Op()
        op.id = len(self.ops)
        op.eng = eng
        op.fn = fn
        op.chan = chan
        op.inc = inc
        op.sig = chan is not None
        op.cnt = 0
        sd, wd = set(), set()
        rk, wk = [], []
        for a in reads:
            rk.extend(self._keys(a))
        for a in writes:
            wk.extend(self._keys(a))
        blk = self.blk
        for k in rk:
            e = blk.get(k)
            if e is not None and e[0] >= 0:
                sd.add(e[0])
        for k in wk:
            e = blk.get(k)
            if e is not None:
                if e[0] >= 0:
                    sd.add(e[0])
                wd.update(e[1].values())
                wd.update(e[2])
        isdma = chan is not None
        for k in rk:
            e = blk.get(k)
            if e is None:
                e = [-1, {}, []]
                blk[k] = e
            if isdma:
                e[2].append(op.id)
            else:
                e[1][eng] = op.id
        for k in wk:
            blk[k] = [op.id, {}, []]
        sd.discard(op.id)
        wd.discard(op.id)
        op.sdeps = sd
        op.wdeps = wd - sd
        if chan is not None:
            self.chan_cnt[chan] = self.chan_cnt.get(chan, 0) + inc
            op.cnt = self.chan_cnt[chan]
        self.ops.append(op)
        return op

    def finalize(self, stack):
        nc = self.nc
        ops = self.ops
        for op in ops:
            best = {}
            for d, strong in [(x, True) for x in op.sdeps] + [(x, False) for x in op.wdeps]:
                p = ops[d]
                if p.chan is not None:
                    key = ("c", p.chan)
                else:
                    if p.eng == op.eng:
                        if p.eng == "pe":
                            continue
                        if not strong and op.chan is None:
                            continue
                    key = ("e", p.eng)
                if key not in best or best[key] < d:
                    best[key] = d
            op.waits = list(best.values())
            for d in op.waits:
                ops[d].sig = True
        ecnt = {}
        for op in ops:
            if op.chan is None and op.sig:
                ecnt[op.eng] = ecnt.get(op.eng, 0) + 1
                op.cnt = ecnt[op.eng]
        for e, c in ecnt.items():
            assert c < 60000, (e, c)
        sems = {}
        for e in ("pe", "act", "dve", "pool", "sp"):
            sems[("e", e)] = stack.enter_context(nc.semaphore("s_" + e))
        for ch in self.chan_cnt:
            sems[("c", ch)] = stack.enter_context(nc.semaphore("c_" + ch))
        streams = {e: [] for e in ("pe", "act", "dve", "pool", "sp")}
        for op in ops:
            streams[op.eng].append(op)

        def emit(ename, eng):
            waited = {}
            for op in streams[ename]:
                for d in op.waits:
                    p = ops[d]
                    key = ("c", p.chan) if p.chan is not None else ("e", p.eng)
                    if waited.get(key, 0) < p.cnt:
                        eng.wait_ge(sems[key], p.cnt)
                        waited[key] = p.cnt
                ins = op.fn(eng)
                if op.chan is not None:
                    ins.then_inc(sems[("c", op.chan)], op.inc)
                elif op.sig:
                    ins.then_inc(sems[("e", op.eng)], 1)
            if ename == "sp":
                for ch in self.final_chans:
                    eng.wait_ge(sems[("c", ch)], self.chan_cnt[ch])

        block = stack.enter_context(nc.Block())

        @block.tensor
        def _(e):
            emit("pe", e)

        @block.scalar
        def _(e):
            emit("act", e)

        @block.vector
        def _(e):
            emit("dve", e)

        @block.gpsimd
        def _(e):
            emit("pool", e)

        @block.sync
        def _(e):
            emit("sp", e)


class Ctx:
    pass


def build_program(stop=None):
    nc = bass.Bass("TRN2", target_bir_lowering=False)
    P = Prog(nc)
    C = Ctx()
    C.nc, C.P = nc, P

    def dram(name, shape, dt, kind):
        return nc.dram_tensor(name, list(shape), dt, kind=kind).ap()

    D = Ctx()
    D.xT = dram("xT", [128, 16, 1024], F32, "ExternalInput")
    D.params = dram("params", [128, NPAR], F32, "ExternalInput")
    D.ident = dram("ident", [128, 128], BF16, "ExternalInput")
    D.pos = dram("pos", [64, 1024], I32, "ExternalInput")
    D.maskrows = dram("maskrows", [16, 4, 1024], BF16, "ExternalInput")
    D.qsel = dram("qsel", [16, 1024], BF16, "ExternalInput")
    dbg = DEBUG_MODE is not None
    D.w1r = dram("w1r", [1, 1, 128, 16, 512] if dbg else [4, 16, 128, 16, 512], F32, "ExternalInput")
    D.w2r = dram("w2r", [1, 1, 128, 4, 2048] if dbg else [4, 16, 128, 4, 2048], F32, "ExternalInput")
    D.pw1r = dram("pw1r", [1, 1, 128, 16, 512] if dbg else [2, 8, 128, 16, 512], F32, "ExternalInput")
    D.pw2r = dram("pw2r", [1, 1, 128, 16, 512] if dbg else [2, 4, 128, 16, 512], F32, "ExternalInput")
    nj = 1 if dbg else 2
    D.winr = dram("winr", [nj, 2, 128, 16, 512], F32, "ExternalInput")
    D.wkper = dram("wkper", [nj, 128, 16, 128], F32, "ExternalInput")
    D.wqr = dram("wqr", [nj, 16, 128, 4, 256], F32, "ExternalInput")
    D.wkvr = dram("wkvr", [nj, 2, 128, 4, 2048], F32, "ExternalInput")
    D.wor = dram("wor", [nj, 4, 128, 16, 512], F32, "ExternalInput")
    D.yT = dram("yT", [128, 16, 1024], F32, "ExternalOutput")
    D.kloc = [dram("kloc%d" % g, [512, 1024], BF16, "Internal") for g in range(4)]
    D.vloc = [dram("vloc%d" % g, [512, 1024], BF16, "Internal") for g in range(4)]
    D.kall = [dram("kall%d" % g, [2048, 1024], BF16, "Internal") for g in range(4)]
    D.vall = [dram("vall%d" % g, [2048, 1024], BF16, "Internal") for g in range(4)]
    D.ploc = dram("ploc", [64, 1024], BF16, "Internal")
    D.pall = dram("pall", [256, 1024], BF16, "Internal")
    D.tl_loc = dram("tl_loc", [256, 512], BF16, "Internal")
    D.tl_all = dram("tl_all", [1024, 512], BF16, "Internal")
    D.kv2loc = [dram("kv2loc%d" % i, [512, 1024], BF16, "Internal") for i in range(2)]
    D.kv2all = [dram("kv2all%d" % i, [2048, 1024], BF16, "Internal") for i in range(2)]
    C.D = D

    TOTAL = 212000
    base, _end = nc.bump_sbuf(TOTAL)
    cnt = [0]

    def sb(shape, dt, off, space="sb"):
        cnt[0] += 1
        h = nc.alloc_sbuf_tensor_at("t%d" % cnt[0], list(shape), dt, offset=base + off)
        P.reg(h, space, off)
        return h

    XO, HO, SMO, ARO = 0, 65536, 98304, 113664
    C.X = sb([128, 16, 1024], F32, XO)
    C.H = sb([128, 16, 1024], BF16, HO)
    C.PAR = sb([128, NPAR], F32, SMO)
    C.IDENT = sb([128, 128], BF16, SMO + 5632)
    C.ONES = sb([128, 128], BF16, SMO + 5888)
    C.COS = sb([128, 1024], F32, SMO + 6144)
    C.SIN = sb([128, 1024], F32, SMO + 10240)
    C.EPS = sb([128, 2], F32, SMO + 14336)
    C.ST = sb([128, 64], F32, SMO + 14400, space="st")
    A = ARO
    NT = A + 92160
    C.SQ = [sb([128, 512], BF16, NT + i * 1024) for i in range(2)]
    C.RS = sb([128, 512], F32, NT + 2048)
    C.RSTD = sb([128, 512], F32, NT + 4096)
    C.W1V = [sb([128, 16, 512], BF16, A + s * 16384) for s in range(4)]
    C.W2V = [sb([128, 4, 2048], BF16, A + s * 16384) for s in range(4)]
    C.WK = sb([128, 16, 128], BF16, A + 32768)
    C.A1 = [sb([128, 4, 1024], BF16, A + 65536 + i * 8192) for i in range(2)]
    C.RT = [sb([128, 512], F32, A + 81920 + i * 2048) for i in range(2)]
    C.U = sb([128, 16, 2, 544], BF16, A + 32768)
    C.DG = [sb([128, 31, 128], BF16, A + 67584 + i * 7936) for i in range(2)]
    C.TL = sb([128, 8, 512], BF16, A + 67584)
    C.SG = [sb([128, 512], F32, A + 83456 + i * 2048) for i in range(2)]
    C.TC = sb([128, 2, 512], BF16, A + 83456)
    C.LNT = [sb([128, 512], F32, A + 83456 + i * 2048) for i in range(4)]
    C.CRAW = sb([128, 8, 512], F32, A + 49152)
    C.CN = sb([128, 8, 1024], BF16, A + 65536)
    C.KS = [sb([128, 1024], BF16, A + 81920 + i * 2048) for i in range(3)]
    C.RT1 = sb([128, 512], F32, A + 88064)
    C.RT2 = sb([128, 512], F32, A + 90112)
    C.KT = [sb([128, 4, 1024], BF16, A + i * 8192) for i in range(2)]
    C.VT = sb([128, 4, 1024], BF16, A + 16384)
    C.VV = sb([128, 32, 128], BF16, A + 24576)
    C.PB = [sb([128, 4096], BF16, A + 32768 + i * 8192) for i in range(2)]
    C.PT = [sb([128, 32, 128], BF16, A + 49152 + i * 8192) for i in range(2)]
    C.KPE = sb([128, 4, 1024], BF16, A + 73728)
    C.WQ = sb([128, 4, 256], BF16, A + 81920)
    C.QN = [sb([128, 1024], BF16, A + 83968 + i * 2048) for i in range(2)]
    C.QP = [sb([128, 1024], BF16, A + 88064 + i * 2048) for i in range(2)]
    C.AT1 = sb([128, 512], F32, A + 92160)
    C.AT2 = sb([128, 512], F32, A + 94208)
    C.DINV = sb([128, 8, 128], BF16, A + 96256)
    C.TMPI = sb([128, 1024], I32, A + 0)
    C.TMPA = sb([128, 1024], F32, A + 4096)
    C.TMPB = sb([128, 1024], F32, A + 8192)
    C.TMPC = sb([128, 1024], F32, A + 12288)
    C.TMPD = sb([128, 1024], F32, A + 16384)
    stack = ExitStack()
    C.PS = []
    C.PSB = []
    for b in range(8):
        h = stack.enter_context(nc.psum_tensor("ps%d" % b, [128, 512], F32))
        P.reg(h, "ps", b * 2048)
        C.PS.append(h)
        hb = h.bitcast(BF16)
        P.reg(hb, "ps", b * 2048)
        C.PSB.append(hb)

    emit_all(C, stop)
    P.finalize(stack)
    stack.close()
    return nc


def mm(C, out, lhsT, rhs, start, stop):
    C.P.add("pe", lambda e: e.matmul(out, lhsT=lhsT, rhs=rhs, start=start, stop=stop),
            reads=[lhsT, rhs], writes=[out])


def tr(C, out, in_, ident):
    C.P.add("pe", lambda e: e.transpose(out, in_, ident), reads=[in_, ident], writes=[out])


def act(C, out, in_, func, bias=None, scale=None, accum=None):
    reads = [in_]
    kw = {}
    if bias is not None:
        kw["bias"] = bias
        if not isinstance(bias, float):
            reads.append(bias)
    if scale is not None:
        kw["scale"] = scale
        if not isinstance(scale, float):
            reads.append(scale)
    writes = [out]
    if accum is not None:
        kw["accum_out"] = accum
        writes.append(accum)
    C.P.add("act", lambda e: e.activation(out=out, in_=in_, func=func, **kw), reads=reads, writes=writes)


def tt(C, out, in0, in1, op, eng="dve"):
    C.P.add(eng, lambda e: e.tensor_tensor(out=out, in0=in0, in1=in1, op=op), reads=[in0, in1], writes=[out])


def stt(C, out, in0, scalar, in1, op0, op1):
    reads = [in0, in1]
    if not isinstance(scalar, float):
        reads.append(scalar)
    C.P.add("dve", lambda e: e.scalar_tensor_tensor(out=out, in0=in0, scalar=scalar, in1=in1, op0=op0, op1=op1),
            reads=reads, writes=[out])


def ts(C, out, in0, s1, op0, s2=None, op1=None):
    reads = [in0]
    if not isinstance(s1, float):
        reads.append(s1)
    if s2 is not None and not isinstance(s2, float):
        reads.append(s2)
    if op1 is None:
        C.P.add("dve", lambda e: e.tensor_scalar(out=out, in0=in0, scalar1=s1, scalar2=None, op0=op0),
                reads=reads, writes=[out])
    else:
        C.P.add("dve", lambda e: e.tensor_scalar(out=out, in0=in0, scalar1=s1, scalar2=s2, op0=op0, op1=op1),
                reads=reads, writes=[out])


def cp(C, out, in_, eng="dve"):
    if eng == "act":
        C.P.add("act", lambda e: e.copy(out=out, in_=in_), reads=[in_], writes=[out])
    else:
        C.P.add(eng, lambda e: e.tensor_copy(out=out, in_=in_), reads=[in_], writes=[out])


def red(C, out, in_, op, negate=False):
    C.P.add("dve", lambda e: e.tensor_reduce(out=out, in_=in_, axis=AX.X, op=op, negate=negate),
            reads=[in_], writes=[out])


def recip(C, out, in_):
    C.P.add("dve", lambda e: e.reciprocal(out=out, in_=in_), reads=[in_], writes=[out])


_dma_n = [0]


def dma(C, out, in_, chan=None, reads=None, writes=None, eng="sp", cast=False):
    if chan is None:
        _dma_n[0] += 1
        chan = "d%d" % _dma_n[0]
    r = reads if reads is not None else [in_]
    w = writes if writes is not None else [out]
    if cast:
        C.P.add("pool", lambda e: e.dma_start(out=out, in_=in_, max_dma_last_dim=8192), reads=r, writes=w, chan=chan)
    else:
        C.P.add(eng, lambda e: e.dma_start(out=out, in_=in_), reads=r, writes=w, chan=chan)
    return chan


def wload(C, view, src, slot):
    dma(C, view[:], src, chan="W%d" % slot, reads=["w"], writes=[view[:]], cast=True)


def par(C, col, n=1):
    return C.PAR[:, col:col + n]


def TS(t):
    return slice(t * 512, (t + 1) * 512)


def startup(C):
    D = C.D
    for q in range(4):
        dma(C, C.X[:, 4 * q:4 * q + 4, :], D.xT[:, 4 * q:4 * q + 4, :], reads=["xin"])
    dma(C, C.PAR[:], D.params, reads=["pin"])
    dma(C, C.IDENT[:], D.ident, reads=["pin"])
    dma(C, C.TMPI[0:64, :], D.pos, reads=["pin"])
    C.P.add("dve", lambda e: e.memset(C.ONES[:], 1.0), writes=[C.ONES[:]])
    C.P.add("dve", lambda e: e.memset(C.EPS[:, 0:1], 1e-6), writes=[C.EPS[:, 0:1]])
    C.P.add("dve", lambda e: e.memset(C.EPS[:, 1:2], 1e-5), writes=[C.EPS[:, 1:2]])
    R = slice(0, 64)
    A_, B_, Cc, Dd = C.TMPA[R, :], C.TMPB[R, :], C.TMPC[R, :], C.TMPD[R, :]
    cp(C, A_, C.TMPI[R, :])
    ts(C, A_, A_, C.PAR[R, RC:RC + 1], ALU.mult)
    ts(C, B_, A_, float(1.0 / TWO_PI), ALU.mult)
    cp(C, C.TMPI[R, :], B_)
    cp(C, B_, C.TMPI[R, :])
    stt(C, Cc, B_, float(-C1), A_, ALU.mult, ALU.add)
    stt(C, Cc, B_, float(-C2), Cc, ALU.mult, ALU.add)

    def wrap(t_, tmp):
        ts(C, tmp, t_, float(np.pi), ALU.is_gt, float(-TWO_PI), ALU.mult)
        tt(C, t_, t_, tmp, ALU.add)
        ts(C, tmp, t_, float(-np.pi), ALU.is_lt, float(TWO_PI), ALU.mult)
        tt(C, t_, t_, tmp, ALU.add)

    wrap(Cc, Dd)
    act(C, B_, Cc, AF.Sin)
    ts(C, C.SIN[R, :], B_, C.PAR[R, RC + 1:RC + 2], ALU.mult)
    ts(C, Cc, Cc, float(np.pi / 2), ALU.add)
    wrap(Cc, Dd)
    act(C, C.COS[R, :], Cc, AF.Sin)


def rmsnorm(C, gcol, final=False, deferred=False):
    for t in range(2):
        ps = C.PS[7 - t] if deferred else C.PS[7]
        if not deferred:
            for c in range(16):
                sq = C.SQ[c % 2]
                if c % 2 == 0:
                    act(C, sq[:], C.X[:, c, TS(t)], AF.Square)
                else:
                    tt(C, sq[:], C.X[:, c, TS(t)], C.X[:, c, TS(t)], ALU.mult)
                mm(C, ps[:], C.ONES[:], sq[:], c == 0, c == 15)
        act(C, C.RS[:], ps[:], AF.Sqrt, bias=C.EPS[:, 0:1], scale=float(1.0 / 2048))
        recip(C, C.RSTD[:], C.RS[:])
        for c in range(16):
            dst = C.X[:, c, TS(t)] if final else C.H[:, c, TS(t)]
            stt(C, dst, C.X[:, c, TS(t)], par(C, gcol + c), C.RSTD[:], ALU.mult, ALU.mult)


class StatAcc:
    def __init__(self, C):
        self.C = C
        self.pend = None
        self.n = 0

    def add(self, c, t):
        C = self.C
        sq = C.SQ[self.n % 2]
        self.n += 1
        act(C, sq[:], C.X[:, c, TS(t)], AF.Square)
        prev = self.pend
        self.pend = (c, t, sq)
        if prev is not None:
            self._mm(prev)

    def _mm(self, p):
        c, t, sq = p
        mm(self.C, self.C.PS[7 - t][:], self.C.ONES[:], sq[:], c == 0, c == 15)

    def flush(self):
        if self.pend is not None:
            self._mm(self.pend)
            self.pend = None


def mlp(C, L):
    D = C.D

    def load(g):
        wload(C, C.W1V[(2 * g) % 4], D.w1r[L, g], (2 * g) % 4)
        wload(C, C.W2V[(2 * g + 1) % 4], D.w2r[L, g], (2 * g + 1) % 4)

    load(0)
    rmsnorm(C, G_MLP + 16 * L, deferred=True)
    sacc = StatAcc(C)
    k1 = [0]
    k2 = [0]

    def w1part(g):
        w1 = C.W1V[(2 * g) % 4]
        A1 = C.A1[g % 2]
        for fc in range(4):
            for t in range(2):
                ps = C.PS[k1[0] % 3]
                rt = C.RT[k1[0] % 2]
                k1[0] += 1
                for kc in range(16):
                    mm(C, ps[:], w1[:, kc, fc * 128:(fc + 1) * 128], C.H[:, kc, TS(t)], kc == 0, kc == 15)
                act(C, rt[:], ps[:], AF.Relu)
                act(C, A1[:, fc, TS(t)], rt[:], AF.Square)

    def w2part(g):
        w2 = C.W2V[(2 * g + 1) % 4]
        A1 = C.A1[g % 2]
        for d2 in range(16):
            for t in range(2):
                ps = C.PS[3 + k2[0] % 3]
                k2[0] += 1
                for kc in range(4):
                    mm(C, ps[:], w2[:, kc, d2 * 128:(d2 + 1) * 128], A1[:, kc, TS(t)], kc == 0, kc == 3)
                tt(C, C.X[:, d2, TS(t)], C.X[:, d2, TS(t)], ps[:], ALU.add)
                if g == 15:
                    sacc.add(d2, t)
        if g == 15:
            sacc.flush()

    wload(C, C.W1V[2], D.w1r[L, 1], 2)
    w1part(0)
    for g in range(16):
        if g + 1 < 16:
            wload(C, C.W2V[(2 * (g + 1) + 1) % 4], D.w2r[L, g + 1], (2 * (g + 1) + 1) % 4)
            w1part(g + 1)
        if g + 2 < 16:
            wload(C, C.W1V[(2 * (g + 2)) % 4], D.w1r[L, g + 2], (2 * (g + 2)) % 4)
        w2part(g)


def proj_residual(C, wsrc, bias_col):
    wload(C, C.W1V[0], wsrc[0], 0)
    sacc = StatAcc(C)
    k = 0
    for n in range(4):
        if n + 1 < 4:
            wload(C, C.W1V[(n + 1) % 2], wsrc[n + 1], (n + 1) % 2)
        w = C.W1V[n % 2]
        for dd in range(4):
            d2 = 4 * n + dd
            for t in range(2):
                ps = C.PS[k % 4]
                k += 1
                for kc in range(16):
                    mm(C, ps[:], w[:, kc, dd * 128:(dd + 1) * 128], C.H[:, kc, TS(t)], kc == 0, kc == 15)
                if bias_col is None:
                    tt(C, C.X[:, d2, TS(t)], C.X[:, d2, TS(t)], ps[:], ALU.add)
                else:
                    stt(C, C.X[:, d2, TS(t)], ps[:], par(C, bias_col + d2), C.X[:, d2, TS(t)], ALU.add, ALU.add)
                sacc.add(d2, t)
    sacc.flush()


def allgather(C, src, dst, skeys, dkey, name):
    if isinstance(skeys, str):
        skeys = [skeys]
    C.P.add("pool", lambda e: e.collective_compute("AllGather", ALU.bypass,
                                                   replica_groups=[[0, 1, 2, 3], [4, 5, 6, 7]],
                                                   ins=[src], outs=[dst]),
            reads=list(skeys), writes=[dkey], chan=name, inc=1)


def conv_a(C, L):
    D = C.D
    j = L // 2
    cb = CB + 96 * j
    wload(C, C.W1V[0], D.pw1r[j, 0], 0)
    wload(C, C.W1V[1], D.pw1r[j, 1], 1)
    rmsnorm(C, G_MIX + 16 * L, deferred=(L > 0))
    k = 0
    for m in range(8):
        if 1 <= m and m + 1 < 8:
            wload(C, C.W1V[(m + 1) % 2], D.pw1r[j, m + 1], (m + 1) % 2)
        w = C.W1V[m % 2]
        for cc in range(2):
            c = 2 * m + cc
            for t in range(2):
                pa = C.PS[k % 2]
                pg = C.PS[2 + k % 2]
                sg = C.SG[k % 2]
                k += 1
                for kc in range(16):
                    mm(C, pa[:], w[:, kc, cc * 128:(cc + 1) * 128], C.H[:, kc, TS(t)], kc == 0, kc == 15)
                for kc in range(16):
                    mm(C, pg[:], w[:, kc, 256 + cc * 128:256 + (cc + 1) * 128], C.H[:, kc, TS(t)], kc == 0, kc == 15)
                act(C, sg[:], pg[:], AF.Sigmoid, bias=par(C, cb + 16 + c))
                stt(C, C.U[:, c, t, 32:544], pa[:], par(C, cb + c), sg[:], ALU.add, ALU.mult)
    for t in range(2):
        cp(C, C.TC[:, t, :].rearrange("p (c j) -> p c j", j=32), C.U[:, :, t, 512:544])
    dma(C, D.tl_loc.rearrange("(s p) n -> p s n", p=128), C.TC[:], writes=["tl_loc"])
    allgather(C, D.tl_loc, D.tl_all, "tl_loc", "tl_all", "agt")


def conv_b(C, L):
    D = C.D
    j = L // 2
    cb = CB + 96 * j
    dma(C, C.TL[:], D.tl_all.rearrange("(r p) n -> p r n", p=128), reads=["tl_all"], chan="tlld")
    for t in range(2):
        halo = C.U[:, :, t, 0:32]
        for cand in range(8):
            src = C.TL[:, cand, :].rearrange("p (c j) -> p c j", j=32)
            sc = par(C, SEL + 8 * t + cand)
            if cand == 0:
                ts(C, halo, src, sc, ALU.mult)
            else:
                stt(C, halo, src, sc, halo, ALU.mult, ALU.add)
    statb = [(C.PS[6], C.PS[7]), (C.PS[2], C.PS[3])]
    pend = []
    kk = 0
    nparr = NPAR
    for c in range(16):
        dg = C.DG[c % 2]
        woff = WDW + (j * 16 + c) * 31
        idb = bass.AP(C.IDENT[:].tensor, C.IDENT[:].offset, [[128, 128], [0, 31], [1, 128]])
        wv = C.PAR[:, woff:woff + 31]
        wb = bass.AP(wv.tensor, wv.offset, [[nparr, 128], [1, 31], [0, 128]])
        C.P.add("dve", (lambda dg_, idb_, wb_: (lambda e: e.tensor_tensor(out=dg_[:], in0=idb_, in1=wb_, op=ALU.mult)))(dg, idb, wb),
                reads=[C.IDENT[:], wv], writes=[dg[:]])
        for t in range(2):
            ps = C.PS[4 + kk % 2]
            sq = C.SQ[kk % 2]
            kk += 1
            for k in range(31):
                mm(C, ps[:], dg[:, k, :], C.U[:, c, t, 2 + k:2 + k + 512], k == 0, k == 30)
            ts(C, C.H[:, c, TS(t)], ps[:], par(C, cb + 32 + c), ALU.add)
            act(C, sq[:], C.H[:, c, TS(t)], AF.Square)
            for (c0, t0, sq0) in pend:
                mm(C, statb[t0][0][:], C.ONES[:], C.H[:, c0, TS(t0)], c0 == 0, c0 == 15)
                mm(C, statb[t0][1][:], C.ONES[:], sq0[:], c0 == 0, c0 == 15)
            pend = [(c, t, sq)]
    for (c0, t0, sq0) in pend:
        mm(C, statb[t0][0][:], C.ONES[:], C.H[:, c0, TS(t0)], c0 == 0, c0 == 15)
        mm(C, statb[t0][1][:], C.ONES[:], sq0[:], c0 == 0, c0 == 15)
    for t in range(2):
        s1, s2 = statb[t]
        mu, var, nmr, t1 = C.LNT[0], C.LNT[1], C.LNT[2], C.LNT[3]
        act(C, mu[:], s1[:], AF.Copy, scale=float(1.0 / 2048))
        tt(C, var[:], mu[:], mu[:], ALU.mult)
        stt(C, var[:], s2[:], float(1.0 / 2048), var[:], ALU.mult, ALU.subtract)
        act(C, C.RS[:], var[:], AF.Sqrt, bias=C.EPS[:, 1:2])
        recip(C, C.RSTD[:], C.RS[:])
        stt(C, nmr[:], mu[:], float(-1.0), C.RSTD[:], ALU.mult, ALU.mult)
        for c in range(16):
            tmp = t1 if c % 2 == 0 else var
            tt(C, tmp[:], C.H[:, c, TS(t)], C.RSTD[:], ALU.mult)
            tt(C, tmp[:], tmp[:], nmr[:], ALU.add)
            act(C, C.H[:, c, TS(t)], tmp[:], AF.Silu, bias=par(C, cb + 64 + c), scale=par(C, cb + 48 + c))
    proj_residual(C, D.pw2r[j], cb + 80)


def sub_rmsnorm(C, t, o0, gcol, psb):
    for i in range(4):
        sq = C.SQ[i % 2]
        act(C, sq[:], C.CRAW[:, o0 + i, :], AF.Square)
        mm(C, psb[:], C.ONES[:], sq[:], i == 0, i == 3)
    act(C, C.RS[:], psb[:], AF.Sqrt, bias=C.EPS[:, 0:1], scale=float(1.0 / 512))
    recip(C, C.RSTD[:], C.RS[:])
    for i in range(4):
        stt(C, C.CN[:, o0 + i, TS(t)], C.CRAW[:, o0 + i, :], par(C, gcol + i), C.RSTD[:], ALU.mult, ALU.mult)


def mla_a(C, L):
    D = C.D
    j = L // 2
    wload(C, C.W1V[0], D.winr[j, 0], 0)
    wload(C, C.W1V[1], D.winr[j, 1], 1)
    dma(C, C.WK[:], D.wkper[j], chan="W2", reads=["w"], writes=[C.WK[:]], cast=True)
    rmsnorm(C, G_MIX + 16 * L, deferred=(L > 0 and DEBUG_MODE is None))
    kc_ = [0]

    def win_chunks(t, o_list):
        for o in o_list:
            w = C.W1V[o // 4]
            oc = o % 4
            ps = C.PS[kc_[0] % 3]
            kc_[0] += 1
            for kc in range(16):
                mm(C, ps[:], w[:, kc, oc * 128:(oc + 1) * 128], C.H[:, kc, TS(t)], kc == 0, kc == 15)
            cp(C, C.CRAW[:, o, :], ps[:], eng="act")

    for t in range(2):
        px, pr = C.PS[3], C.PS[4]
        for kc in range(16):
            mm(C, px[0:64, :], C.WK[:, kc, 0:64], C.H[:, kc, TS(t)], kc == 0, kc == 15)
        for kc in range(16):
            mm(C, pr[0:64, :], C.WK[:, kc, 64:128], C.H[:, kc, TS(t)], kc == 0, kc == 15)
        tt(C, C.RT1[0:64, :], px[0:64, :], C.COS[0:64, TS(t)], ALU.mult)
        tt(C, C.RT2[0:64, :], pr[0:64, :], C.SIN[0:64, TS(t)], ALU.mult)
        tt(C, C.KS[2][0:64, TS(t)], C.RT1[0:64, :], C.RT2[0:64, :], ALU.add)
    dma(C, D.ploc, C.KS[2][0:64, :], writes=["ploc"], chan="ks2")
    allgather(C, D.ploc, D.pall, "ploc", "pall", "agp")
    wload(C, C.W2V[2], D.wkvr[j, 0], 2)
    for t in range(2):
        win_chunks(t, range(4, 8))
        sub_rmsnorm(C, t, 4, MG + 8 + 4 * j, C.PS[7])
    cnt = {"k": 0, "n": 0}

    def kv_one(h, which, w, dst_ap, key):
        hc = (h % 8) * 256
        ks = C.KS[cnt["n"] % 2]
        for t in range(2):
            ps = C.PS[cnt["k"] % 3]
            cnt["k"] += 1
            for kc in range(4):
                mm(C, ps[:], w[:, kc, hc + which * 128:hc + (which + 1) * 128], C.CN[:, 4 + kc, TS(t)], kc == 0, kc == 3)
            cp(C, ks[:, TS(t)], ps[:], eng=("act" if (cnt["k"] % 2) else "dve"))
        dma(C, dst_ap, ks[:], writes=[key], chan="ks%d" % (cnt["n"] % 2))
        cnt["n"] += 1

    def kv_group(g4):
        w = C.W2V[2] if g4 < 2 else C.W2V[0]
        if g4 == 0:
            for c2 in range(2):
                keys = []
                for which in range(2):
                    for hh in range(2):
                        h = 2 * c2 + hh
                        r0 = which * 256 + hh * 128
                        key = "kv2loc%d_%d" % (c2, which * 2 + hh)
                        keys.append(key)
                        kv_one(h, which, w, D.kv2loc[c2][r0:r0 + 128, :], key)
                allgather(C, D.kv2loc[c2], D.kv2all[c2], keys, "kv2all%d" % c2, "agk")
            return
        for which in range(2):
            for hh in range(4):
                h = 4 * g4 + hh
                dst = (D.kloc if which == 0 else D.vloc)[g4]
                kv_one(h, which, w, dst[hh * 128:(hh + 1) * 128, :], "%sloc%d_%d" % ("kv"[which], g4, hh))
            if which == 0:
                allgather(C, D.kloc[g4], D.kall[g4], ["kloc%d_%d" % (g4, q) for q in range(4)], "kall%d" % g4, "agk")
            else:
                allgather(C, D.vloc[g4], D.vall[g4], ["vloc%d_%d" % (g4, q) for q in range(4)], "vall%d" % g4, "agk")

    kv_group(0)
    for t in range(2):
        win_chunks(t, range(0, 4))
        sub_rmsnorm(C, t, 0, MG + 4 * j, C.PS[6])
    wload(C, C.W2V[0], D.wkvr[j, 1], 0)
    for g4 in range(1, 4):
        kv_group(g4)


def q_head(C, L, h, step=None):
    D = C.D
    j = L // 2
    groups = []

    def g_load():
        dma(C, C.WQ[:], D.wqr[j, h], chan="WQ", reads=["w"], writes=[C.WQ[:]], cast=True)

    def g_nope(t):
        ps = C.PS[7]
        for kc in range(4):
            mm(C, ps[:], C.WQ[:, kc, 0:128], C.CN[:, kc, TS(t)], kc == 0, kc == 3)
        act(C, C.QN[h % 2][:, TS(t)], ps[:], AF.Copy, scale=float(SCALE))

    def g_pe(t):
        px, pr = C.PS[6], C.PS[7]
        for kc in range(4):
            mm(C, px[0:64, :], C.WQ[:, kc, 128:192], C.CN[:, kc, TS(t)], kc == 0, kc == 3)
        for kc in range(4):
            mm(C, pr[0:64, :], C.WQ[:, kc, 192:256], C.CN[:, kc, TS(t)], kc == 0, kc == 3)
        stt(C, C.AT1[0:64, :], px[0:64, :], float(SCALE), C.COS[0:64, TS(t)], ALU.mult, ALU.mult)
        stt(C, C.AT2[0:64, :], pr[0:64, :], float(SCALE), C.SIN[0:64, TS(t)], ALU.mult, ALU.mult)
        tt(C, C.QP[h % 2][0:64, TS(t)], C.AT1[0:64, :], C.AT2[0:64, :], ALU.add)

    groups.append(g_load)
    for t in range(2):
        groups.append(lambda t=t: g_nope(t))
        groups.append(lambda t=t: g_pe(t))
    return groups


def attention(C, L):
    D = C.D
    def ksrc(h):
        if h < 4:
            return D.kv2all[h // 2].rearrange("(r n) t -> n r t", r=4)[(h % 2) * 128:(h % 2 + 1) * 128]
        return D.kall[h // 4].rearrange("(r n) t -> n r t", r=4)[(h % 4) * 128:(h % 4 + 1) * 128]

    def vsrc(h):
        if h < 4:
            return D.kv2all[h // 2].rearrange("(r n) t -> n r t", r=4)[256 + (h % 2) * 128:256 + (h % 2 + 1) * 128]
        return D.vall[h // 4].rearrange("(r n) t -> n r t", r=4)[(h % 4) * 128:(h % 4 + 1) * 128]

    def kkey(h):
        return ("kv2all%d" % (h // 2)) if h < 4 else ("kall%d" % (h // 4))

    def vkey(h):
        return ("kv2all%d" % (h // 2)) if h < 4 else ("vall%d" % (h // 4))

    dma(C, C.KPE[64:80, :, :], D.maskrows, reads=["pin"], chan="mrow")
    dma(C, C.QP[0][64:80, :], D.qsel, reads=["pin"], chan="qsel0")
    dma(C, C.QP[1][64:80, :], D.qsel, reads=["pin"], chan="qsel1")
    for g in q_head(C, L, 0):
        g()
    dma(C, C.KPE[0:64, :, :], D.pall.rearrange("(r n) t -> n r t", r=4), reads=["pall"], chan="kpe")
    dma(C, C.KT[0][:], ksrc(0), reads=[kkey(0)], chan="kt0")
    dma(C, C.VT[:], vsrc(0), reads=[vkey(0)], chan="vt")
    sctr = [0]
    ictr = [0]
    octr = [0]
    ST = C.ST
    iters = [(h, tile, qt) for h in range(16) for tile in range(2) for qt in range(4)]

    def nstage(itx):
        return 4 if iters[itx][1] == 0 else 8

    def rs_of(tile, s):
        return (s, 0) if tile == 0 else (s // 2, s % 2)

    def S_stage(itx, s):
        h, tile, qt = iters[itx]
        pbi = itx % 2
        qcol = slice(tile * 512 + qt * 128, tile * 512 + (qt + 1) * 128)
        kt = C.KT[h % 2]
        pb = C.PB[pbi]
        so = 16 * pbi
        r, slot = rs_of(tile, s)
        ps = C.PS[sctr[0] % 3]
        sctr[0] += 1
        mm(C, ps[:], C.QN[h % 2][:, qcol], kt[:, r, slot * 512:(slot + 1) * 512], True, False)
        mm(C, ps[:], C.QP[h % 2][0:80, qcol], C.KPE[0:80, r, slot * 512:(slot + 1) * 512], False, True)
        red(C, ST[:, so + s:so + s + 1], ps[:], ALU.max, negate=True)
        act(C, pb[:, s * 512:(s + 1) * 512], ps[:], AF.Exp, bias=ST[:, so + s:so + s + 1],
            accum=ST[:, so + 8 + s:so + 8 + s + 1])

    def combine(itx):
        ns = nstage(itx)
        pbi = itx % 2
        so = 16 * pbi
        negm = ST[:, so:so + ns]
        lsum = ST[:, so + 8:so + 8 + ns]
        o2 = 32 + 12 * pbi
        nmg = ST[:, o2:o2 + 1]
        ee = ST[:, o2 + 2:o2 + 2 + ns]
        red(C, nmg, negm, ALU.min)
        act(C, ee, negm, AF.Exp, bias=nmg, scale=-1.0)
        tt(C, lsum, lsum, ee, ALU.mult)
        red(C, ST[:, o2 + 1:o2 + 2], lsum, ALU.add)
        recip(C, ST[:, o2 + 1:o2 + 2], ST[:, o2 + 1:o2 + 2])
        ts(C, ee, ee, ST[:, o2 + 1:o2 + 2], ALU.mult)

    def build_dinv(itx):
        ns = nstage(itx)
        pbi = itx % 2
        o2 = 32 + 12 * pbi
        ee = ST[:, o2 + 2:o2 + 2 + ns]
        idv = C.IDENT[:]
        idb = bass.AP(idv.tensor, idv.offset, [[128, 128], [0, ns], [1, 128]])
        eb = bass.AP(ee.tensor, ee.offset, [[64, 128], [1, ns], [0, 128]])
        dv = C.DINV[:, 0:ns, :]
        C.P.add("dve", lambda e: e.tensor_tensor(out=dv, in0=idb, in1=eb, op=ALU.mult), reads=[idv, ee], writes=[dv])

    def T_stage(itx, s):
        pbi = itx % 2
        pb = C.PB[pbi]
        pt = C.PT[pbi]
        ps = C.PS[3 + (ictr[0] % 2)]
        ictr[0] += 1
        for sub in range(4):
            mm(C, ps[:, sub * 128:(sub + 1) * 128], pb[:, s * 512 + sub * 128:s * 512 + (sub + 1) * 128],
               C.DINV[:, s, :], True, True)
        cp(C, pt[:, 4 * s:4 * s + 4, :], ps[:].rearrange("p (a b) -> p a b", b=128),
           eng=("dve" if s % 4 == 0 else "act"))

    def PV_stage(itx, s, po):
        h, tile, qt = iters[itx]
        pt = C.PT[itx % 2]
        ns = nstage(itx)
        r, slot = rs_of(tile, s)
        for sub in range(4):
            kt_ = 4 * s + sub
            vidx = r * 8 + slot * 4 + sub
            mm(C, po, C.VV[:, vidx, :], pt[:, kt_, :], kt_ == 0, kt_ == 4 * ns - 1)

    def vtrans(h):
        for r in range(4):
            pv = C.PSB[6]
            for q8 in range(8):
                tr(C, pv[:, q8 * 128:(q8 + 1) * 128], C.VT[:, r, q8 * 128:(q8 + 1) * 128], C.IDENT[:])
            cp(C, C.VV[:, r * 8:(r + 1) * 8, :], pv[:].rearrange("p (a b) -> p a b", b=128),
               eng=("act" if r % 2 else "dve"))
        if h + 1 < 16:
            dma(C, C.VT[:], vsrc(h + 1), reads=[vkey(h + 1)], chan="vt")

    dma(C, C.KT[1][:], ksrc(1), reads=[kkey(1)], chan="kt1")
    vtrans(0)
    for s in range(nstage(0)):
        S_stage(0, s)
    combine(0)
    build_dinv(0)
    qg = q_head(C, L, 1)
    N = len(iters)
    for itx in range(N):
        h, tile, qt = iters[itx]
        first_of_head = (tile == 0 and qt == 0)
        if first_of_head and h >= 1:
            vtrans(h)
            if h + 1 < 16:
                dma(C, C.KT[(h + 1) % 2][:], ksrc(h + 1), reads=[kkey(h + 1)], chan="kt%d" % ((h + 1) % 2))
                qg = q_head(C, L, h + 1)
            else:
                qg = []
        ns = nstage(itx)
        nn = nstage(itx + 1) if itx + 1 < N else 0
        oslot = octr[0] % 4
        octr[0] += 1
        po = C.PS[5][:, oslot * 128:(oslot + 1) * 128]
        LAG = 2
        SL = 1
        for s in range(max(ns + LAG + SL, nn)):
            if s < nn:
                S_stage(itx + 1, s)
                if s == nn - 1:
                    combine(itx + 1)
            if SL <= s < ns + SL:
                T_stage(itx, s - SL)
            if LAG + SL <= s < ns + LAG + SL:
                PV_stage(itx, s - LAG - SL, po)
        cp(C, C.H[:, h, tile * 512 + qt * 128:tile * 512 + (qt + 1) * 128], po, eng="act")
        if itx + 1 < N:
            build_dinv(itx + 1)
        li = tile * 4 + qt
        if 1 <= li <= len(qg):
            qg[li - 1]()


def mla_b(C, L):
    D = C.D
    j = L // 2
    attention(C, L)
    proj_residual(C, D.wor[j], None)


def emit_all(C, stop=None):
    D = C.D
    phases = []
    startup(C)
    seq = [
        lambda: conv_a(C, 0), lambda: conv_b(C, 0), lambda: mlp(C, 0),
        lambda: mla_a(C, 1), lambda: mla_b(C, 1), lambda: mlp(C, 1),
        lambda: conv_a(C, 2), lambda: conv_b(C, 2), lambda: mlp(C, 2),
        lambda: mla_a(C, 3), lambda: mla_b(C, 3), lambda: mlp(C, 3),
    ]
    if DEBUG_MODE == "mla":
        seq = [lambda: mla_a(C, 1), lambda: mla_b(C, 1)]
        stop = 2
    n = len(seq) if stop is None else stop
    for idx, f in enumerate(seq[:n]):
        f()
        if (idx + 1) in DEBUG_DUMPS:
            dd = C.nc.dram_tensor("dbg%d" % (idx + 1), [128, 16, 1024], F32, kind="ExternalOutput").ap()
            for q in range(4):
                ch = dma(C, dd[:, 4 * q:4 * q + 4, :], C.X[:, 4 * q:4 * q + 4, :], writes=["dbgout%d" % idx])
                C.P.final_chans.append(ch)
    if stop is None:
        rmsnorm(C, G_FIN, final=True, deferred=True)
    for q in range(4):
        ch = dma(C, D.yT[:, 4 * q:4 * q + 4, :], C.X[:, 4 * q:4 * q + 4, :], writes=["yout"])
        C.P.final_chans.append(ch)


def _core_tokens(i):
    a = np.arange(i * 512, (i + 1) * 512)
    b = np.arange((7 - i) * 512, (8 - i) * 512)
    return np.concatenate([a, b])


def prepare_inputs(inp):
    f = np.float32
    x = np.asarray(inp["x"], f)
    pos = np.asarray(inp["positions"]).astype(np.int32)
    shared = {}
    w1 = np.asarray(inp["mlp_w1"], f)
    shared["w1r"] = np.ascontiguousarray(w1.reshape(4, 16, 128, 16, 512).transpose(0, 3, 2, 1, 4))
    w2 = np.asarray(inp["mlp_w2"], f)
    shared["w2r"] = np.ascontiguousarray(w2.reshape(4, 16, 4, 128, 2048).transpose(0, 1, 3, 2, 4))
    pw1 = np.asarray(inp["conv_w_pw1"], f)
    t = pw1.reshape(2, 16, 128, 2, 8, 256).transpose(0, 4, 2, 1, 3, 5)
    shared["pw1r"] = np.ascontiguousarray(t).reshape(2, 8, 128, 16, 512)
    pw2 = np.asarray(inp["conv_w_pw2"], f)
    shared["pw2r"] = np.ascontiguousarray(pw2.reshape(2, 16, 128, 4, 512).transpose(0, 3, 2, 1, 4))
    win = np.asarray(inp["mla_w_in"], f)
    t = win[:, :, :1024].reshape(2, 16, 128, 2, 512).transpose(0, 3, 2, 1, 4)
    shared["winr"] = np.ascontiguousarray(t)
    kp = win[:, :, 1024:1088]
    kpp = np.concatenate([kp, kp[:, :, 32:64], kp[:, :, 0:32]], axis=2)
    shared["wkper"] = np.ascontiguousarray(kpp.reshape(2, 16, 128, 128).transpose(0, 2, 1, 3))
    wq = np.asarray(inp["mla_w_q_up"], f).reshape(2, 4, 128, 16, 192)
    wq2 = np.concatenate([wq, wq[..., 160:192], wq[..., 128:160]], axis=-1)
    shared["wqr"] = np.ascontiguousarray(wq2.transpose(0, 3, 2, 1, 4))
    wkv = np.asarray(inp["mla_w_kv_up"], f).reshape(2, 4, 128, 2, 2048)
    shared["wkvr"] = np.ascontiguousarray(wkv.transpose(0, 3, 2, 1, 4))
    wo = np.asarray(inp["mla_w_o"], f)
    shared["wor"] = np.ascontiguousarray(wo.reshape(2, 16, 128, 4, 512).transpose(0, 3, 2, 1, 4))
    shared["ident"] = np.eye(128, dtype=f).astype(ml_dtypes.bfloat16)
    qsel = np.zeros((16, 1024), f)
    for qi in range(8):
        for hf in range(2):
            qsel[2 * qi + hf, qi * 128 + hf * 64: qi * 128 + hf * 64 + 64] = 1.0
    shared["qsel"] = qsel.astype(ml_dtypes.bfloat16)

    def fm(v):
        return np.asarray(v, f).reshape(16, 128).T

    base = np.zeros((128, NPAR), f)
    for L in range(4):
        base[:, G_MIX + 16 * L:G_MIX + 16 * L + 16] = fm(inp["norm_mixer_g"][L])
        base[:, G_MLP + 16 * L:G_MLP + 16 * L + 16] = fm(inp["norm_mlp_g"][L])
    base[:, G_FIN:G_FIN + 16] = fm(inp["final_norm_g"])
    for j in range(2):
        cb = CB + 96 * j
        b1 = np.asarray(inp["conv_b_pw1"][j], f)
        base[:, cb:cb + 16] = fm(b1[:2048])
        base[:, cb + 16:cb + 32] = fm(b1[2048:])
        base[:, cb + 32:cb + 48] = fm(inp["conv_b_dw"][j])
        base[:, cb + 48:cb + 64] = fm(inp["conv_ln_g"][j])
        base[:, cb + 64:cb + 80] = fm(inp["conv_ln_b"][j])
        base[:, cb + 80:cb + 96] = fm(inp["conv_b_pw2"][j])
        wd = np.asarray(inp["conv_w_dw"][j], f)
        base[:, WDW + j * 496:WDW + (j + 1) * 496] = wd.reshape(31, 16, 128).transpose(2, 1, 0).reshape(128, 496)
        base[:, MG + 4 * j:MG + 4 * j + 4] = np.asarray(inp["mla_q_norm_g"][j], f).reshape(4, 128).T
        base[:, MG + 8 + 4 * j:MG + 8 + 4 * j + 4] = np.asarray(inp["mla_kv_norm_g"][j], f).reshape(4, 128).T
    invf = (np.float32(10000.0) ** (-np.arange(0, 64, 2, dtype=f) / np.float32(64))).astype(f)
    base[0:64, RC] = np.concatenate([invf, invf])
    base[0:64, RC + 1] = np.concatenate([-np.ones(32, f), np.ones(32, f)])

    if DEBUG_MODE is not None:
        for k in ("w1r", "w2r", "pw1r", "pw2r"):
            shared[k] = np.ascontiguousarray(shared[k][:1, :1])
        for k in ("winr", "wkper", "wqr", "wkvr", "wor"):
            shared[k] = np.ascontiguousarray(shared[k][:1])
    maps = []
    for core in range(8):
        b, i = core // 4, core % 4
        tok = _core_tokens(i)
        m = dict(shared)
        m["xT"] = np.ascontiguousarray(x[b, tok, :].T.reshape(16, 128, 1024).transpose(1, 0, 2))
        m["pos"] = np.ascontiguousarray(np.broadcast_to(pos[b, tok][None, :], (64, 1024))).astype(np.int32)
        pr = base.copy()
        selA = np.zeros(8, f)
        selB = np.zeros(8, f)
        if i >= 1:
            selA[(i - 1) * 2 + 0] = 1.0
        if i < 3:
            selB[(i + 1) * 2 + 1] = 1.0
        else:
            selB[3 * 2 + 0] = 1.0
        pr[:, SEL:SEL + 8] = selA[None, :]
        pr[:, SEL + 8:SEL + 16] = selB[None, :]
        m["params"] = pr
        ktok = np.zeros((4, 1024), np.int64)
        for r in range(4):
            ktok[r] = _core_tokens(r)
        kch = ktok // 64
        mr = np.zeros((16, 4, 1024), f)
        for qi in range(8):
            for hf in range(2):
                qtok = tok[qi * 128 + hf * 64]
                qch = qtok // 64
                mr[2 * qi + hf] = np.where(kch > qch, NEG, 0.0)
        m["maskrows"] = mr.astype(ml_dtypes.bfloat16)
        maps.append(m)
    return maps


_NC_CACHE = {}


def _gather(res, name):
    out = np.zeros((2, 4096, 2048), np.float32)
    for core in range(8):
        b, i = core // 4, core % 4
        tok = _core_tokens(i)
        y = np.asarray(res.results[core][name])
        out[b, tok, :] = y.transpose(2, 1, 0).reshape(1024, 2048)
    return out


_LAST = {}


def kernel(**inputs):
    import time, sys
    t0 = time.time()
    maps = prepare_inputs(inputs)
    t1 = time.time()
    if "nc" not in _NC_CACHE:
        _NC_CACHE["nc"] = build_program(DEBUG_STOP)
    nc = _NC_CACHE["nc"]
    t2 = time.time()
    if DEBUG_MODE is not None:
        res = run_bass_kernel_spmd(nc, maps, core_ids=list(range(8)), trace=True)
        print("DEBUG exec_time_ns", res.exec_time_ns, file=sys.stderr)
    else:
        res = run_bass_kernel_spmd(nc, maps, core_ids=list(range(8)))
    t3 = time.time()
    print("kernel timing: prep %.1f build %.1f run %.1f" % (t1 - t0, t2 - t1, t3 - t2), file=sys.stderr)
    _LAST["res"] = res
    return _gather(res, "yT")
```
